# Optimizing a Trainium2 kernel written in Bass

```python
import jax, jax.numpy as jnp
from jax import lax
import numpy as np

D_MODEL = 1024
BATCH = 4
SEQ = 4096
DEPTH = 1

CHUNK = 64
PLE_DIM = 256
D_FF = 2816
MIX_WIDTH = D_MODEL
GLA_HEADS = 4
GLA_WIDTH = MIX_WIDTH // 2
GLA_DV = GLA_WIDTH // GLA_HEADS
GLA_DK = GLA_DV // 2
GLA_QK = GLA_HEADS * GLA_DK
GLA_GATE_RANK = 16
GLA_GATE_NORM = 16.0
RET_HEADS = 4
RET_WIDTH = MIX_WIDTH - GLA_WIDTH
RET_DV = RET_WIDTH // RET_HEADS
RET_DK = RET_DV // 2
RET_QK = RET_HEADS * RET_DK
ROPE_BASE = 10000.0
EPS = 1e-6
IN_COLS = 2 * GLA_QK + 2 * GLA_WIDTH + GLA_GATE_RANK + 2 * RET_QK + 2 * RET_WIDTH

kernel_name = "hymba_gla_retnet_macaron_ple"


def rms_norm(x, g):
    xf = x.astype(jnp.float32)
    y = xf * lax.rsqrt(jnp.mean(xf * xf, axis=-1, keepdims=True) + EPS)
    return (y * g.astype(jnp.float32)).astype(x.dtype)


def swiglu(x, w_gate, w_up, w_down):
    return (jax.nn.silu(x @ w_gate) * (x @ w_up)) @ w_down


def to_chunks(t, heads, dim):
    b, s, _ = t.shape
    return t.reshape(b, s // CHUNK, CHUNK, heads, dim).transpose(0, 3, 1, 2, 4)


def from_chunks(t):
    b, h, nc, c, d = t.shape
    return t.transpose(0, 2, 3, 1, 4).reshape(b, nc * c, h, d)


def chunk_state_scan(local_state, chunk_decay):
    def step(state, inp):
        u, a = inp
        return a * state + u, state
    xs = (jnp.moveaxis(local_state, 2, 0), jnp.moveaxis(chunk_decay, 2, 0))
    _, prev = lax.scan(step, jnp.zeros_like(local_state[:, :, 0]), xs)
    return jnp.moveaxis(prev, 0, 2)


def rope(t, heads, dim):
    b, s, _ = t.shape
    half = dim // 2
    inv = ROPE_BASE ** (-jnp.arange(half, dtype=jnp.float32) / half)
    ang = jnp.arange(s, dtype=jnp.float32)[:, None] * inv[None, :]
    cos = jnp.cos(ang)[None, :, None, :].astype(t.dtype)
    sin = jnp.sin(ang)[None, :, None, :].astype(t.dtype)
    th = t.reshape(b, s, heads, dim)
    t1, t2 = th[..., :half], th[..., half:]
    out = jnp.concatenate([t1 * cos - t2 * sin, t2 * cos + t1 * sin], axis=-1)
    return out.reshape(b, s, heads * dim)


def gla_mixer(q, k, v, r, gate_lr, w_gate_up, b_gate, g_norm):
    dt = q.dtype
    b_, s_, _ = q.shape
    logit = gate_lr @ w_gate_up + b_gate
    log_a = jax.nn.log_sigmoid(logit.astype(jnp.float32)) / GLA_GATE_NORM
    cum = jnp.cumsum(to_chunks(log_a, GLA_HEADS, GLA_DK), axis=3)
    cum_last = cum[:, :, :, -1:, :]
    qc = to_chunks(q, GLA_HEADS, GLA_DK) * (GLA_DK ** -0.5)
    kc = to_chunks(k, GLA_HEADS, GLA_DK)
    vc = to_chunks(v, GLA_HEADS, GLA_DV)
    e_pos = jnp.exp(cum).astype(dt)
    e_neg = jnp.exp(-cum).astype(dt)
    q_fwd = qc * e_pos
    idx = jnp.arange(CHUNK)
    causal = idx[:, None] >= idx[None, :]
    s_fwd = jnp.einsum('bhnid,bhnjd->bhnij', q_fwd, kc * e_neg)
    s_bwd = jnp.einsum('bhnid,bhnjd->bhnij', qc * e_neg, kc * e_pos)
    scores = jnp.where(causal, s_fwd, s_bwd)
    o_intra = jnp.einsum('bhnij,bhnjv->bhniv', scores, vc)
    k_state = kc * jnp.exp(cum_last - cum).astype(dt)
    local = jnp.einsum('bhnjd,bhnjv->bhndv', k_state, vc)
    decay = jnp.exp(cum_last[:, :, :, 0, :])[..., None].astype(dt)
    prev = chunk_state_scan(local, decay)
    o_inter = jnp.einsum('bhnid,bhndv->bhniv', q_fwd, prev)
    o = rms_norm(from_chunks(o_intra + o_inter), g_norm.reshape(GLA_HEADS, GLA_DV))
    return o.reshape(b_, s_, GLA_WIDTH) * jax.nn.silu(r)


def retention_mixer(q, k, v, g, g_norm):
    dt = q.dtype
    b_, s_, _ = q.shape
    log_gamma = jnp.log(1.0 - 2.0 ** (-5.0 - jnp.arange(RET_HEADS, dtype=jnp.float32)))
    qc = to_chunks(rope(q, RET_HEADS, RET_DK), RET_HEADS, RET_DK) * (RET_DK ** -0.5)
    kc = to_chunks(rope(k, RET_HEADS, RET_DK), RET_HEADS, RET_DK)
    vc = to_chunks(v, RET_HEADS, RET_DV)
    pos = jnp.arange(CHUNK, dtype=jnp.float32)
    dist = jnp.abs(pos[:, None] - pos[None, :])
    dmask = jnp.exp(log_gamma[:, None, None] * dist)[None, :, None].astype(dt)
    scores = jnp.einsum('bhnid,bhnjd->bhnij', qc, kc) * dmask
    o_intra = jnp.einsum('bhnij,bhnjv->bhniv', scores, vc)
    k_dec = jnp.exp(log_gamma[:, None] * (CHUNK - 1 - pos)[None, :])[None, :, None, :, None].astype(dt)
    local = jnp.einsum('bhnjd,bhnjv->bhndv', kc * k_dec, vc)
    decay = jnp.broadcast_to(jnp.exp(log_gamma * CHUNK)[None, :, None, None, None].astype(dt),
                             local.shape[:-1] + (1,))
    prev = chunk_state_scan(local, decay)
    q_dec = jnp.exp(log_gamma[:, None] * (pos + 1.0)[None, :])[None, :, None, :, None].astype(dt)
    o_inter = jnp.einsum('bhnid,bhndv->bhniv', qc * q_dec, prev)
    o = rms_norm(from_chunks(o_intra + o_inter), g_norm.reshape(RET_HEADS, RET_DV))
    return o.reshape(b_, s_, RET_WIDTH) * jax.nn.silu(g)


def setup_inputs(seed: int = 0) -> dict:
    key = jax.random.key(seed)
    ks = jax.random.split(key, 21)
    f32 = jnp.float32

    def w(k, shape, fan_in):
        return jax.random.normal(k, shape, f32) * (fan_in ** -0.5)

    def gain(k, shape):
        return 1.0 + 0.02 * jax.random.normal(k, shape, f32)

    L = DEPTH
    return {
        "x": jax.random.normal(ks[0], (BATCH, SEQ, D_MODEL), f32),
        "p": jax.random.normal(ks[1], (DEPTH, BATCH, SEQ, PLE_DIM), f32),
        "g_ffn1": gain(ks[2], (L, D_MODEL)),
        "w_ffn1_gate": w(ks[3], (L, D_MODEL, D_FF), D_MODEL),
        "w_ffn1_up": w(ks[4], (L, D_MODEL, D_FF), D_MODEL),
        "w_ffn1_down": w(ks[5], (L, D_FF, D_MODEL), D_FF),
        "g_mix": gain(ks[6], (L, D_MODEL)),
        "w_in": w(ks[7], (L, D_MODEL, IN_COLS), D_MODEL),
        "w_gla_gate_up": w(ks[8], (L, GLA_GATE_RANK, GLA_QK), GLA_GATE_RANK),
        "b_gla_gate": 0.1 * jax.random.normal(ks[9], (L, GLA_QK), f32),
        "g_gla_out": gain(ks[10], (L, GLA_WIDTH)),
        "g_ret_out": gain(ks[11], (L, RET_WIDTH)),
        "w_out": w(ks[12], (L, MIX_WIDTH, D_MODEL), MIX_WIDTH),
        "g_ffn2": gain(ks[13], (L, D_MODEL)),
        "w_ffn2_gate": w(ks[14], (L, D_MODEL, D_FF), D_MODEL),
        "w_ffn2_up": w(ks[15], (L, D_MODEL, D_FF), D_MODEL),
        "w_ffn2_down": w(ks[16], (L, D_FF, D_MODEL), D_FF),
        "g_ple": gain(ks[17], (L, D_MODEL)),
        "w_ple_gate": w(ks[18], (L, D_MODEL, D_MODEL), D_MODEL),
        "w_ple_proj": w(ks[19], (L, PLE_DIM, D_MODEL), PLE_DIM),
        "g_final": gain(ks[20], (D_MODEL,)),
    }


def reference(x, p, g_ffn1, w_ffn1_gate, w_ffn1_up, w_ffn1_down, g_mix, w_in, w_gla_gate_up,
              b_gla_gate, g_gla_out, g_ret_out, w_out, g_ffn2, w_ffn2_gate, w_ffn2_up, w_ffn2_down,
              g_ple, w_ple_gate, w_ple_proj, g_final):
    sizes = [GLA_QK, GLA_QK, GLA_WIDTH, GLA_WIDTH, GLA_GATE_RANK, RET_QK, RET_QK, RET_WIDTH, RET_WIDTH]
    offsets = [int(o) for o in np.cumsum(sizes)[:-1]]
    h = x
    for i in range(DEPTH):
        h = h + 0.5 * swiglu(rms_norm(h, g_ffn1[i]), w_ffn1_gate[i], w_ffn1_up[i], w_ffn1_down[i])
        z = rms_norm(h, g_mix[i]) @ w_in[i]
        gq, gk, gv, gr, glr, rq, rk, rv, rg = jnp.split(z, offsets, axis=-1)
        y_gla = gla_mixer(gq, gk, gv, gr, glr, w_gla_gate_up[i], b_gla_gate[i], g_gla_out[i])
        y_ret = retention_mixer(rq, rk, rv, rg, g_ret_out[i])
        h = h + jnp.concatenate([y_gla, y_ret], axis=-1) @ w_out[i]
        h = h + 0.5 * swiglu(rms_norm(h, g_ffn2[i]), w_ffn2_gate[i], w_ffn2_up[i], w_ffn2_down[i])
        gate = jax.nn.sigmoid(rms_norm(h, g_ple[i]) @ w_ple_gate[i])
        h = h + gate * (p[i] @ w_ple_proj[i])
    return rms_norm(h, g_final)
```

```python
import math
from contextlib import ExitStack

import numpy as np
import ml_dtypes
import concourse.bass as bass
import concourse.mybir as mybir
from concourse.bass_utils import run_bass_kernel_spmd

F32 = mybir.dt.float32
BF16 = mybir.dt.bfloat16
U8 = mybir.dt.uint8
ALU = mybir.AluOpType
AF = mybir.ActivationFunctionType

ENGS = ["pe", "act", "dve", "pool", "sp"]


class Buf:
    __slots__ = ("name", "last_w", "readers", "sem", "cnt")

    def __init__(self, name):
        self.name = name
        self.last_w = None
        self.readers = []
        self.sem = None
        self.cnt = 0


class Op:
    __slots__ = ("eng", "fn", "deps", "is_dma", "chan", "sem", "val", "has_dep")

    def __init__(self, eng, fn, is_dma):
        self.eng = eng
        self.fn = fn
        self.deps = []
        self.is_dma = is_dma
        self.chan = None
        self.sem = None
        self.val = 0
        self.has_dep = False


class Prog:
    def __init__(self, nc, same_engine_sync=True):
        self.nc = nc
        self.streams = {e: [] for e in ENGS}
        self.same_engine_sync = same_engine_sync
        self.chans = []

    def _need_sync(self, a, b):
        if a.is_dma:
            if b.is_dma and a.chan is b.chan:
                return False
            return True
        if b.is_dma:
            return True
        if a.eng == b.eng:
            if a.eng == "pe":
                return False
            return self.same_engine_sync
        return True

    def _track(self, o, reads, writes):
        deps = []
        for r in reads:
            a = r.last_w
            if a is not None and a is not o and self._need_sync(a, o):
                deps.append(a)
        for w in writes:
            a = w.last_w
            if a is not None and a is not o and self._need_sync(a, o):
                deps.append(a)
            for a in w.readers:
                if a is not o and self._need_sync(a, o):
                    deps.append(a)
        for r in reads:
            r.readers.append(o)
        for w in writes:
            w.last_w = o
            w.readers = []
        seen = set()
        for a in deps:
            if id(a) in seen:
                continue
            seen.add(id(a))
            a.has_dep = True
            o.deps.append((a, a.chan.cnt if a.is_dma else None))

    def op(self, eng, fn, reads=(), writes=()):
        o = Op(eng, fn, False)
        self._track(o, reads, writes)
        self.streams[eng].append(o)
        return o

    def dma(self, eng, fn, reads=(), writes=(), chan=None):
        o = Op(eng, fn, True)
        if chan is None:
            chan = writes[0]
        o.chan = chan
        if chan.sem is None:
            chan.sem = "pending"
            self.chans.append(chan)
        chan.cnt += 16
        o.val = chan.cnt
        self._track(o, reads, writes)
        self.streams[eng].append(o)
        return o

    def emit(self, final_wait_chans=()):
        nc = self.nc
        with ExitStack() as es:
            esem = {e: es.enter_context(nc.semaphore("s_" + e)) for e in ENGS}
            for c in self.chans:
                c.sem = es.enter_context(nc.semaphore("d_" + c.name))
            for e in ENGS:
                t = 0
                for o in self.streams[e]:
                    if o.is_dma:
                        o.sem = o.chan.sem
                    elif o.has_dep:
                        t += 1
                        o.val = t
                        o.sem = esem[e]
            block = es.enter_context(nc.Block())

            def run(ename, eng):
                waited = {}
                for o in self.streams[ename]:
                    for d, dval in o.deps:
                        key = id(d.sem)
                        val = d.val if dval is None else dval
                        if waited.get(key, 0) >= val:
                            continue
                        eng.wait_ge(d.sem, val)
                        waited[key] = val
                    ins = o.fn(eng)
                    if o.is_dma:
                        ins.then_inc(o.sem, 16)
                    elif o.has_dep:
                        ins.then_inc(o.sem, 1)
                if ename == "sp":
                    for c in final_wait_chans:
                        eng.wait_ge(c.sem, c.cnt)

            @block.tensor
            def _(eng):
                run("pe", eng)

            @block.scalar
            def _(eng):
                run("act", eng)

            @block.vector
            def _(eng):
                run("dve", eng)

            @block.gpsimd
            def _(eng):
                run("pool", eng)

            @block.sync
            def _(eng):
                run("sp", eng)


D = 1024
SEQ = 4096
T = 2048
NT = 4
TW = 512
KC = 8
DFF = 2816
NBLK = 11
THIRDS = [(0, 4), (4, 8), (8, 11)]
PLE = 256
CH = 64
EPS = 1e-6
O_GQ, O_GK, O_GV, O_GR, O_GLR, O_RQ, O_RK, O_RV, O_RG = 0, 256, 512, 1024, 1536, 1552, 1808, 2064, 2576
IN_COLS = 3088

OFF_H = 0
OFF_N = 65536
OFF_A = 98304
OFF_WGU = 131072
OFF_WD = 147456
ARENA = 163840


def build_program(with_state_pass=True):
    nc = bass.Bass("TRN2", target_bir_lowering=False)
    P = Prog(nc, same_engine_sync=False)

    def din(name, shape, dt=F32):
        return nc.dram_tensor(name, list(shape), dt, kind="ExternalInput").ap()

    xa = din("xa", [D, T])
    xb = din("xb", [D, T])
    pT = din("pT", [PLE, T])
    w1g, w1u, w1d = din("w1g", [D, DFF]), din("w1u", [D, DFF]), din("w1d", [DFF, D])
    w2g, w2u, w2d = din("w2g", [D, DFF]), din("w2u", [D, DFF]), din("w2d", [DFF, D])
    win = din("win", [D, IN_COLS])
    wupa = din("wupa", [17, 256])
    wout = din("wout", [D, D])
    wpg = din("wpg", [D, D])
    wpp = din("wpp", [PLE, D])
    gains_d = din("gains", [128, 40])
    ghead_d = din("ghead", [128, 8])
    flag_d = din("flag", [128, 1])
    csa_d = din("csa", [2, 128, T])
    csb_d = din("csb", [2, 128, T])
    mfb_d = din("mfb", [128, 2, 64])
    dm_d = din("dm", [128, 4, 64])
    gq_d = din("gq", [128, 2, 64])
    dectok_d = din("dectok", [128, 2, 128])
    dret_d = din("dret", [128, 2])
    mscan_d = din("mscan", [128, 512])
    ident_d = din("ident", [128, 128], BF16)
    outT = nc.dram_tensor("outT", [D, T], F32, kind="ExternalOutput").ap()

    es = ExitStack()
    with es:
        def sb(name, shape, dt):
            return es.enter_context(nc.sbuf_tensor("sb_" + name, list(shape), dt))

        arena = sb("arena", [128, ARENA], U8)

        def carve(off, nbytes, dt):
            return arena[:, off:off + nbytes].bitcast(dt)

        h = carve(OFF_H, 65536, F32).rearrange("p (k t) -> p k t", k=KC)
        n = carve(OFF_N, 32768, BF16).rearrange("p (k t) -> p k t", k=KC)
        a = carve(OFF_A, 32768, BF16).rearrange("p (k t) -> p k t", k=KC)
        wgu = [carve(OFF_WGU + s * 8192, 8192, BF16).rearrange("p (g k c) -> p g k c", g=2, k=KC) for s in range(2)]
        wd = carve(OFF_WD, 16384, BF16).rearrange("p (k c) -> p k c", k=KC)
        stg = [carve(OFF_A + i * 2048, 2048, F32) for i in range(8)]
        mw = [carve(OFF_WGU, 16384, BF16).rearrange("p (k c) -> p k c", k=KC), wd]
        ptb = carve(OFF_WGU, 8192, BF16).rearrange("p (k t) -> p k t", k=2)

        gains = sb("gains", [128, 40], F32)
        ghead = sb("ghead", [128, 8], F32)
        flag = sb("flag", [128, 1], F32)
        mfb = sb("mfb", [128, 2, 64], F32)
        dm = sb("dm", [128, 4, 64], F32)
        gq = sb("gq", [128, 2, 64], F32)
        dectok = sb("dectok", [128, 2, 128], F32)
        dret = sb("dret", [128, 2], F32)
        mscan = sb("mscan", [128, 512], F32)
        ident = sb("ident", [128, 128], BF16)
        ones = sb("ones", [128, 128], BF16)
        wupa_sb = sb("wupa", [32, 256], F32)
        wglr = sb("wglr", [128, KC, 16], BF16)
        sqr = [sb(f"sq{i}", [128, TW], BF16) for i in range(2)]
        rstd = [sb(f"rstd{i}", [128, TW], F32) for i in range(1)]
        sil = [sb(f"sil{i}", [128, TW], BF16) for i in range(2)]
        f32all = sb("f32w", [128, 5, TW], F32)
        f32w = [f32all[:, i, :] for i in range(5)]
        b16w = [sb(f"b16w{i}", [128, TW], BF16) for i in range(7)]
        cs = f32all[:, 2:4, :]
        glra = sb("glra", [32, TW], F32)
        v_sbs = [sb(f"v_sb{i}", [128, 4, 256], BF16) for i in range(2)]
        qi_t = [sb(f"qi{i}", [128, TW], BF16) for i in range(2)]
        kst_sb = sb("kst_sb", [128, 4, 128], BF16)
        S_all1 = sb("S_all", [128, 9, 128], F32)
        S_all = [S_all1] * 4
        S16 = sb("S16", [128, 8, 128], BF16)
        S_in = sb("S_in", [128, 4, 128], F32)
        ps = [es.enter_context(nc.psum_tensor(f"ps{i}", [128, TW], F32)) for i in range(8)]

        b_h = [[Buf(f"h{k}_{t}") for t in range(NT)] for k in range(KC)]
        b_n = [[Buf(f"n{k}_{t}") for t in range(NT)] for k in range(KC)]
        b_a = [[Buf(f"a{k}_{t}") for t in range(NT)] for k in range(KC)]
        b_wgu = [Buf("wgu0"), Buf("wgu1")]
        b_wd = Buf("wd")
        b_mw = [Buf("mw0"), Buf("mw1")]
        b_ptb = Buf("ptb")
        b_ps = [Buf(f"ps{i}") for i in range(8)]
        b_sq = [Buf(f"sq{i}") for i in range(2)]
        b_rstd = [Buf(f"rstd{i}") for i in range(1)]
        b_sil = [Buf(f"sil{i}") for i in range(2)]
        b_f32w = [Buf(f"f32w{i}") for i in range(5)]
        b_b16w = [Buf(f"b16w{i}") for i in range(7)]
        b_glra = Buf("glra")
        b_vsb = [Buf("v_sb0"), Buf("v_sb1")]
        b_qi = [Buf("qi0"), Buf("qi1")]
        b_cs = Buf("cs")
        b_kst = Buf("kst_sb")
        b_S = [Buf("S")] * 4
        b_S16 = Buf("S16")
        b_Sin = Buf("S_in")
        b_const = Buf("const")
        b_out = Buf("out")
        b_stg = [Buf(f"stg{i}") for i in range(8)]
        cnt = {"bank": 0, "sq": 0, "rstd": 0, "sil": 0, "blk": 0, "ost": 0, "ple": 0}

        def nxt(key, mod):
            v = cnt[key] % mod
            cnt[key] += 1
            return v

        def bank():
            return nxt("bank", 8)

        def mm(out, lhsT, rhs, start, stop, reads, writes):
            P.op("pe", lambda e: e.matmul(out, lhsT=lhsT, rhs=rhs, start=start, stop=stop), reads, writes)

        def tr(out, in_, reads, writes):
            P.op("pe", lambda e: e.transpose(out, in_, ident[:]), list(reads) + [b_const], writes)

        def act(out, in_, func, reads, writes, scale=1.0, bias=0.0):
            P.op("act", lambda e: e.activation(out=out, in_=in_, func=func, bias=bias, scale=scale), reads, writes)

        def tt(eng, out, in0, in1, op, reads, writes):
            P.op(eng, lambda e: e.tensor_tensor(out=out, in0=in0, in1=in1, op=op), reads, writes)

        def stt(eng, out, in0, scalar, in1, op0, op1, reads, writes):
            P.op(eng, lambda e: e.scalar_tensor_tensor(out=out, in0=in0, scalar=scalar, in1=in1, op0=op0, op1=op1),
                 reads, writes)

        def cp(eng, out, in_, reads, writes):
            if eng == "act":
                P.op("act", lambda e: e.copy(out=out, in_=in_), reads, writes)
            else:
                P.op(eng, lambda e: e.tensor_copy(out=out, in_=in_), reads, writes)

        def tsm(eng, out, in0, scalar, reads, writes):
            P.op(eng, lambda e: e.tensor_scalar_mul(out=out, in0=in0, scalar1=scalar), reads, writes)

        def cdma(out, in_, reads, writes, chan=None):
            P.dma("pool", lambda e: e.dma_start(out=out, in_=in_), reads, writes, chan)

        def hdma(out, in_, reads, writes, chan=None):
            P.dma("sp", lambda e: e.dma_start(out=out, in_=in_), reads, writes, chan)

        def tok(t):
            return slice(t * TW, (t + 1) * TW)

        def load_consts():
            cbufs = [Buf(f"c{i}") for i in range(16)]
            consts = [(gains, gains_d), (ghead, ghead_d), (flag, flag_d), (mfb, mfb_d), (dm, dm_d), (gq, gq_d),
                      (dectok, dectok_d), (dret, dret_d), (mscan, mscan_d), (ident, ident_d)]
            for i, (s_, d_) in enumerate(consts):
                hdma(s_[:], d_, [], [cbufs[i]], chan=cbufs[i])
            P.op("pool", lambda e: e.memset(wupa_sb[:], 0.0), [], [cbufs[10]])
            hdma(wupa_sb[0:17, :], wupa, [], [cbufs[10]], chan=cbufs[10])
            P.op("pool", lambda e: e.memset(ones[:], 1.0), [], [cbufs[12]])
            cdma(wglr[:], win[:, O_GLR:O_GLR + 16].rearrange("(k p) c -> p k c", p=128), [], [cbufs[11]],
                 chan=cbufs[11])
            P.op("pool", lambda e: e.memset(glra[:], 1.0), cbufs[0:13], [b_glra, b_const])
            P.op("pool", lambda e: e.memset(S_in[:], 0.0), [], [b_Sin])
            P.op("pool", lambda e: e.memset(S_all1[:], 0.0), [], [b_S[0]])

        b_hl = [Buf(f"hl{t}") for t in range(NT)]

        def load_x(xsrc):
            xv = xsrc.rearrange("(k p) t -> p k t", p=128)
            for t in range(NT):
                hdma(h[:, :, tok(t)], xv[:, :, tok(t)], [], [b_hl[t]] + [b_h[k][t] for k in range(KC)], chan=b_hl[t])

        def norm_phase(gi, final=False):
            for t in range(NT):
                norm_tile(gi, t, final)

        def norm_tile(gi, t, final=False):
            if True:
                bi = bank()
                for k in range(KC):
                    si = nxt("sq", 2)
                    act(sqr[si][:], h[:, k, tok(t)], AF.Square, [b_h[k][t]], [b_sq[si]])
                    mm(ps[bi][:], ones[:], sqr[si][:], k == 0, k == KC - 1, [b_sq[si], b_const], [b_ps[bi]])
                ri = nxt("rstd", 1)
                act(rstd[ri][:], ps[bi][:], AF.Ln, [b_ps[bi]], [b_rstd[ri]], scale=1.0 / D, bias=EPS)
                act(rstd[ri][:], rstd[ri][:], AF.Exp, [b_rstd[ri]], [b_rstd[ri]], scale=-0.5)
                for k in range(KC):
                    g_ap = gains[:, gi * 8 + k:gi * 8 + k + 1]
                    if not final:
                        eng = "dve"
                        stt(eng, n[:, k, tok(t)], h[:, k, tok(t)], g_ap, rstd[ri][:], ALU.mult, ALU.mult,
                            [b_h[k][t], b_rstd[ri], b_const], [b_n[k][t]])
                    else:
                        stt("dve", stg[k], h[:, k, tok(t)], g_ap, rstd[ri][:], ALU.mult, ALU.mult,
                            [b_h[k][t], b_rstd[ri], b_const],
                            [b_stg[k], b_a[k // 2][2 * (k % 2)], b_a[k // 2][2 * (k % 2) + 1]])
                        hdma(outT[k * 128:(k + 1) * 128, tok(t)], stg[k], [b_stg[k]], [b_out], chan=b_stg[k])

        def ffn_loaders(wg, wu, wdn):
            def load_blk(b):
                s = b % 2
                cdma(wgu[s][:, 0], wg[:, b * 256:(b + 1) * 256].rearrange("(k p) c -> p k c", p=128), [],
                     [b_wgu[s], b_mw[0], b_ptb], chan=b_wgu[s])
                cdma(wgu[s][:, 1], wu[:, b * 256:(b + 1) * 256].rearrange("(k p) c -> p k c", p=128), [],
                     [b_wgu[s]], chan=b_wgu[s])

            def load_wd(k3):
                b0, b1 = THIRDS[k3]
                nch = (b1 - b0) * 2
                cdma(wd[:, 0:nch, :], wdn[b0 * 256:b1 * 256, :].rearrange("(c p) d -> p c d", p=128), [],
                     [b_wd, b_mw[1]], chan=b_wd)

            return load_blk, load_wd

        def ffn_phase(wg, wu, wdn, after_tile=None, after_upgate=None, preloaded=False):
            load_blk, load_wd = ffn_loaders(wg, wu, wdn)
            if not preloaded:
                load_blk(0)
                load_blk(1)
            load_wd(0)
            for k3, (b0, b1) in enumerate(THIRDS):
                nch = (b1 - b0) * 2
                for b in range(b0, b1):
                    s = b % 2
                    for j in range(2):
                        ci = (b - b0) * 2 + j
                        for t in range(NT):
                            bu = bank()
                            bv = bank()
                            for k in range(KC):
                                mm(ps[bu][:], wgu[s][:, 0, k, j * 128:(j + 1) * 128], n[:, k, tok(t)], k == 0,
                                   k == KC - 1, [b_wgu[s], b_n[k][t]], [b_ps[bu]])
                            for k in range(KC):
                                mm(ps[bv][:], wgu[s][:, 1, k, j * 128:(j + 1) * 128], n[:, k, tok(t)], k == 0,
                                   k == KC - 1, [b_wgu[s], b_n[k][t]], [b_ps[bv]])
                            si = nxt("sil", 2)
                            act(sil[si][:], ps[bu][:], AF.Silu, [b_ps[bu]], [b_sil[si]])
                            tt("dve", a[:, ci, tok(t)], sil[si][:], ps[bv][:], ALU.mult, [b_sil[si], b_ps[bv]],
                               [b_a[ci][t]])
                    if b + 2 < NBLK:
                        load_blk(b + 2)
                last = k3 + 1 == len(THIRDS)
                if last and after_upgate is not None:
                    after_upgate()
                order = [(dc, t) for t in range(NT) for dc in range(KC)] if last else \
                        [(dc, t) for dc in range(KC) for t in range(NT)]
                for dc, t in order:
                    bk = bank()
                    for ci in range(nch):
                        mm(ps[bk][:], wd[:, ci, dc * 128:(dc + 1) * 128], a[:, ci, tok(t)], ci == 0,
                           ci == nch - 1, [b_wd, b_a[ci][t]], [b_ps[bk]])
                    stt("dve", h[:, dc, tok(t)], ps[bk][:], 0.5, h[:, dc, tok(t)], ALU.mult, ALU.add,
                        [b_ps[bk], b_h[dc][t]], [b_h[dc][t]])
                    if last and dc == KC - 1 and after_tile is not None:
                        after_tile(t)
                if k3 + 1 < len(THIRDS):
                    load_wd(k3 + 1)

        def load_mixer_w(hp, slot):
            m = mw[slot]
            wr = [b_mw[slot]] + ([b_wgu[0], b_wgu[1], b_ptb] if slot == 0 else [b_wd])
            ch = b_mw[slot]

            def ld(c0, ncol, src0):
                cdma(m[:, :, c0:c0 + ncol], win[:, src0:src0 + ncol].rearrange("(k p) c -> p k c", p=128), [], wr,
                     chan=ch)

            if hp < 2:
                ld(0, 128, O_GQ + hp * 128)
                ld(128, 128, O_GK + hp * 128)
                ld(256, 256, O_GV + hp * 256)
                ld(512, 256, O_GR + hp * 256)
            else:
                r = hp - 2
                ld(0, 128, O_RQ + r * 128)
                ld(128, 128, O_RK + r * 128)
                ld(256, 256, O_RV + r * 256)
                ld(512, 256, O_RG + r * 256)
                for dst0, src0 in ((768, O_RQ + r * 128), (896, O_RK + r * 128)):
                    dview = m[:, :, dst0:dst0 + 128].rearrange("p k (hh two j) -> p k hh two j", hh=2, two=2)
                    sview = win[:, src0:src0 + 128].rearrange("(k p) (hh two j) -> p k hh two j", p=128, hh=2, two=2)
                    for two in range(2):
                        for hh in range(2):
                            P.dma("pool", lambda e, dv=dview[:, :, hh, two, :], sv=sview[:, :, hh, 1 - two, :]:
                                  e.dma_start(out=dv, in_=sv), [], wr, ch)

        def proj(m, slot, c0, t):
            bi = bank()
            for k in range(KC):
                mm(ps[bi][:], m[:, k, c0:c0 + 128], n[:, k, tok(t)], k == 0, k == KC - 1,
                   [b_mw[slot], b_n[k][t]], [b_ps[bi]])
            return bi

        def mixer_stages(it, hp, t, slot, state_only, cs_d):
            m = mw[slot]
            is_gla = hp < 2
            r = hp - 2
            W = b16w
            bW = b_b16w
            Fw = f32w
            bF = b_f32w
            par = it % 2
            vs = v_sbs[par]
            b_vs = b_vsb[par]
            QIt = qi_t[par]
            b_QI = b_qi[par]
            QA, KA, QB, KB, KST = 0, 1, 2, 3, 4
            SC0, SC1 = 5, 6
            Sa = S_all1
            bSa = b_S[0]
            def slot_of(c):
                return c if t % 2 == 0 else 8 - c

            if is_gla:
                def dec_ap(c, rows):
                    return Fw[3][rows, c * CH + CH - 1:c * CH + CH]
                b_dec = bF[3]
            else:
                def dec_ap(c, rows):
                    return dret[rows, r:r + 1]
                b_dec = b_const

            def s0():
                for half in range(2):
                    bi = bank()
                    for sub2 in range(2):
                        sub = half * 2 + sub2
                        for k in range(KC):
                            mm(ps[bi][:, sub2 * 256:(sub2 + 1) * 256],
                               n[:, k, t * TW + sub * 128:t * TW + (sub + 1) * 128],
                               m[:, k, 256:512], k == 0, k == KC - 1, [b_mw[slot], b_n[k][t]], [b_ps[bi]])
                    cp("act", vs[:, half * 2:half * 2 + 2, :], ps[bi][:].rearrange("p (s c) -> p s c", s=2),
                       [b_ps[bi]], [b_vs])
                if is_gla:
                    bg = bank()
                    for k in range(KC):
                        mm(ps[bg][0:16, :], wglr[:, k, :], n[:, k, tok(t)], k == 0, k == KC - 1, [b_const, b_n[k][t]],
                           [b_ps[bg]])
                    cp("act", glra[0:16, :], ps[bg][0:16, :], [b_ps[bg]], [b_glra])

            def s1():
                if is_gla:
                    bl = bank()
                    mm(ps[bl][:], wupa_sb[0:17, hp * 128:(hp + 1) * 128], glra[0:17, :], True, True,
                       [b_const, b_glra], [b_ps[bl]])
                    act(Fw[0][:], ps[bl][:], AF.Exp, [b_ps[bl]], [bF[0]], scale=-1.0)
                    act(Fw[0][:], Fw[0][:], AF.Ln, [bF[0]], [bF[0]], bias=1.0)
                    P.op("dve", lambda e: e.tensor_tensor_scan(out=Fw[1][:], data0=mscan[:], data1=Fw[0][:],
                                                               initial=0.0, op0=ALU.mult, op1=ALU.add),
                         [bF[0], b_const], [bF[1]])
                    cn3 = Fw[1][:].rearrange("p (c j) -> p c j", j=CH)
                    tt("pool", Fw[2][:].rearrange("p (c j) -> p c j", j=CH), cn3,
                       cn3[:, :, CH - 1:CH].to_broadcast([128, 8, CH]), ALU.subtract, [bF[1]], [bF[2]])
                    act(Fw[2][:], Fw[2][:], AF.Exp, [bF[2]], [bF[2]], scale=1.0 / 16)
                    act(Fw[3][:], Fw[1][:], AF.Exp, [bF[1]], [bF[3]], scale=-1.0 / 16)
                    if not state_only:
                        act(Fw[0][:], Fw[1][:], AF.Exp, [bF[1]], [bF[0]], scale=1.0 / 16)

            def s1b():
                if is_gla:
                    bk_ = proj(m, slot, 128, t)
                    tt("dve", W[KST][:], ps[bk_][:], Fw[2][:], ALU.mult, [b_ps[bk_], bF[2]], [bW[KST]])
                    if not state_only:
                        tt("dve", W[KA][:], ps[bk_][:], Fw[0][:], ALU.mult, [b_ps[bk_], bF[0]], [bW[KA]])
                        tt("dve", W[KB][:], ps[bk_][:], Fw[3][:], ALU.mult, [b_ps[bk_], bF[3]], [bW[KB]])
                        bq_ = proj(m, slot, 0, t)
                        stt("dve", QIt[:], ps[bq_][:], 0.125, Fw[3][:], ALU.mult, ALU.mult, [b_ps[bq_], bF[3]],
                            [b_QI])
                        stt("dve", W[QB][:], ps[bq_][:], 0.125, Fw[0][:], ALU.mult, ALU.mult, [b_ps[bq_], bF[0]],
                            [bW[QB]])
                else:
                    hdma(cs, cs_d[:, :, tok(t)].rearrange("a p t -> p a t"), [], [b_cs, bF[2], bF[3]], chan=b_cs)

                    def rope(c_main, c_perm, dst, bdst):
                        b1 = proj(m, slot, c_main, t)
                        tt("dve", Fw[0][:], ps[b1][:], cs[:, 0, :], ALU.mult, [b_ps[b1], b_cs], [bF[0]])
                        b2 = proj(m, slot, c_perm, t)
                        tt("dve", Fw[1][:], ps[b2][:], cs[:, 1, :], ALU.mult, [b_ps[b2], b_cs], [bF[1]])
                        tt("pool", dst[:], Fw[0][:], Fw[1][:], ALU.add, [bF[0], bF[1]], [bdst])

                    rope(128, 896, W[KA], bW[KA])
                    if not state_only:
                        rope(0, 768, W[QA], bW[QA])
                        tt("pool", QIt[:].rearrange("p (c j) -> p c j", j=CH),
                           W[QA][:].rearrange("p (c j) -> p c j", j=CH),
                           gq[:, r:r + 1, :].to_broadcast([128, 8, CH]), ALU.mult, [bW[QA], b_const], [b_QI])

            def s2a():
                kst_src = W[KST] if is_gla else W[KA]
                b_kst_src = bW[KST] if is_gla else bW[KA]
                if t == 0:
                    if state_only:
                        P.op("dve", lambda e: e.memset(Sa[:, 0, :], 0.0), [], [bSa])
                    else:
                        tsm("dve", Sa[:, 0, :], S_in[:, hp, :], flag[:, 0:1], [b_Sin, b_const], [bSa])
                bt = bank()
                psb = ps[bt][:].bitcast(BF16)
                for sub in range(4):
                    tr(psb[:, sub * 128:(sub + 1) * 128], kst_src[:, sub * 128:(sub + 1) * 128], [b_kst_src],
                       [b_ps[bt]])
                kview = psb[:, 0:512].rearrange("p (s c) -> p s c", s=4)
                if is_gla:
                    cp("dve", kst_sb[:], kview, [b_ps[bt]], [b_kst])
                else:
                    tt("dve", kst_sb[:], kview, dectok[:, r:r + 1, :].to_broadcast([128, 4, 128]), ALU.mult,
                       [b_ps[bt], b_const], [b_kst])
                if not state_only:
                    qsrc = QIt if is_gla else W[QA]
                    b_qsrc = b_QI if is_gla else bW[QA]
                    for hh in range(2):
                        bs = bank()
                        rows = slice(hh * 64, (hh + 1) * 64)
                        for c in range(8):
                            pb = (c % 2) * 64
                            col = (c // 2) * 64
                            cols = slice(c * CH, (c + 1) * CH)
                            mm(ps[bs][pb:pb + 64, col:col + 64], W[KA][rows, cols], qsrc[rows, cols], True, True,
                               [bW[KA], b_qsrc], [b_ps[bs]])
                            if is_gla:
                                mm(ps[bs][pb:pb + 64, 256 + col:256 + col + 64], W[KB][rows, cols], W[QB][rows, cols],
                                   True, True, [bW[KB], bW[QB]], [b_ps[bs]])
                        sct = W[SC0 + hh]
                        if is_gla:
                            tt("dve", sct[:].rearrange("p (d c i) -> p d c i", d=2, i=CH),
                               ps[bs][:].rearrange("p (d c i) -> p d c i", d=2, i=CH),
                               mfb[:].rearrange("p (d o) i -> p d o i", o=1).to_broadcast([128, 2, 4, CH]), ALU.mult,
                               [b_ps[bs], b_const], [bW[SC0 + hh]])
                        else:
                            hd = r * 2 + hh
                            tt("dve", sct[:, 0:256].rearrange("p (c i) -> p c i", i=CH),
                               ps[bs][:, 0:256].rearrange("p (c i) -> p c i", i=CH),
                               dm[:, hd:hd + 1, :].to_broadcast([128, 4, CH]), ALU.mult, [b_ps[bs], b_const],
                               [bW[SC0 + hh]])
            def s2b(mid=None):
                for rnd in range(2):
                    if rnd == 1 and mid is not None:
                        mid()
                    be, bo = bank(), bank()
                    for cc in range(4):
                        c = rnd * 4 + cc
                        pb = (c % 2) * 64
                        sub = c // 2
                        bU = be if c % 2 == 0 else bo
                        col = (cc // 2) * 256
                        mm(ps[bU][:, col:col + 256], kst_sb[pb:pb + 64, sub, :], vs[pb:pb + 64, sub, :], True, True,
                           [b_kst, b_vs], [b_ps[bU]])
                    for cc in range(4):
                        c = rnd * 4 + cc
                        bU = be if c % 2 == 0 else bo
                        col = (cc // 2) * 256
                        for hh in range(2):
                            rows = slice(hh * 64, (hh + 1) * 64)
                            stt("dve", Sa[rows, slot_of(c + 1), :], Sa[rows, slot_of(c), :], dec_ap(c, rows),
                                ps[bU][rows, col + hh * 128:col + (hh + 1) * 128], ALU.mult, ALU.add,
                                [bSa, b_ps[bU], b_dec], [bSa])
                if not state_only:
                    lo = 0 if t % 2 == 0 else 1
                    cp("dve", S16[:], Sa[:, lo:lo + 8, :], [bSa], [b_S16])
                if state_only and t == NT - 1:
                    cp("pool", S_in[:, hp, :], Sa[:, 0, :], [bSa], [b_Sin])

            keep = {}

            def s3a():
                bie, bio = bank(), bank()
                for hh in range(2):
                    sct = W[SC0 + hh]
                    b_sct = bW[SC0 + hh]
                    for c in range(8):
                        pb = (c % 2) * 64
                        sub = c // 2
                        bI = bie if c % 2 == 0 else bio
                        ocol = hh * 256 + (c // 2) * 64
                        scol = (c // 2) * 64
                        lhs = vs[pb:pb + 64, sub, hh * 128:(hh + 1) * 128]
                        mm(ps[bI][:, ocol:ocol + 64], lhs, sct[pb:pb + 64, scol:scol + 64], True, not is_gla,
                           [b_vs, b_sct], [b_ps[bI]])
                        if is_gla:
                            mm(ps[bI][:, ocol:ocol + 64], lhs, sct[pb:pb + 64, 256 + scol:256 + scol + 64], False,
                               True, [b_vs, b_sct], [b_ps[bI]])
                bn = []
                for hh in range(2):
                    bn_ = bank()
                    bn.append(bn_)
                    rows = slice(hh * 64, (hh + 1) * 64)
                    for c in range(8):
                        mm(ps[bn_][:, c * CH:(c + 1) * CH], S16[rows, c if t % 2 == 0 else 7 - c, :],
                           QIt[rows, c * CH:(c + 1) * CH], True,
                           True, [b_S16, b_QI], [b_ps[bn_]])
                for hh in range(2):
                    ot, b_ot = (Fw[4], bF[4]) if hh == 0 else (rstd[0], b_rstd[0])
                    cp("act", ot[:], ps[bn[hh]][:], [b_ps[bn[hh]]], [b_ot])
                    o4 = ot[:].rearrange("p (cc two i) -> p cc two i", two=2, i=CH)
                    for pr, bI in ((0, bie), (1, bio)):
                        iv = ps[bI][:, hh * 256:(hh + 1) * 256].rearrange("p (cc i) -> p cc i", i=CH)
                        tt("dve", o4[:, :, pr, :], o4[:, :, pr, :], iv, ALU.add, [b_ot, b_ps[bI]], [b_ot])
                    si = nxt("sq", 2)
                    act(sqr[si][:], ot[:], AF.Square, [b_ot], [b_sq[si]])
                    keep["sq", hh] = si

            def sg():
                bgt = [proj(m, slot, 512 + hh * 128, t) for hh in range(2)]
                for hh in range(2):
                    head = hp * 2 + hh
                    act(a[:, head, tok(t)], ps[bgt[hh]][:], AF.Silu, [b_ps[bgt[hh]]], [b_a[head][t]])

            def s3b1():
                for hh in range(2):
                    si = keep["sq", hh]
                    bss = bank()
                    mm(ps[bss][:], ones[:], sqr[si][:], True, True, [b_sq[si], b_const], [b_ps[bss]])
                    act(ps[bss][:], ps[bss][:], AF.Ln, [b_ps[bss]], [b_ps[bss]], scale=1.0 / 128, bias=EPS)
                    act(ps[bss][:], ps[bss][:], AF.Exp, [b_ps[bss]], [b_ps[bss]], scale=-0.5)
                    keep["bss", hh] = bss

            def s3b2():
                for hh in range(2):
                    head = hp * 2 + hh
                    ot, b_ot = (Fw[4], bF[4]) if hh == 0 else (rstd[0], b_rstd[0])
                    bss = keep["bss", hh]
                    stt("dve", ot[:], ot[:], ghead[:, head:head + 1], ps[bss][:], ALU.mult, ALU.mult,
                        [b_ot, b_ps[bss], b_const], [b_ot])

            def s3b3():
                for hh in range(2):
                    head = hp * 2 + hh
                    ot, b_ot = (Fw[4], bF[4]) if hh == 0 else (rstd[0], b_rstd[0])
                    tt("pool", a[:, head, tok(t)], ot[:], a[:, head, tok(t)], ALU.mult, [b_ot, b_a[head][t]],
                       [b_a[head][t]])

            return dict(s0=s0, s1=s1, s1b=s1b, s2a=s2a, s2b=s2b, sg=sg, s3a=s3a, s3b1=s3b1, s3b2=s3b2, s3b3=s3b3)

        def mixer_pass(state_only, cs_d, preloaded0=False, on_slot0_free=None):
            if not preloaded0:
                load_mixer_w(0, 0)
            load_mixer_w(1, 1)
            its = [(hp, t) for hp in range(4) for t in range(NT)]
            st = [mixer_stages(i, hp, t, hp % 2, state_only, cs_d) for i, (hp, t) in enumerate(its)]
            nI = len(its)

            def finish(i):
                hp, t = its[i]
                if t == NT - 1 and hp + 2 < 4:
                    load_mixer_w(hp + 2, hp % 2)
                if t == NT - 1 and hp == 2 and on_slot0_free is not None:
                    on_slot0_free()

            st[0]["s0"]()
            st[0]["s1"]()
            st[0]["s1b"]()
            full = not state_only
            for i in range(nI):
                if i + 1 < nI:
                    st[i + 1]["s0"]()
                st[i]["s2a"]()
                if full:
                    st[i]["sg"]()
                if i >= 1 and full:
                    st[i - 1]["s3b1"]()
                if i >= 1 and full:
                    st[i]["s2b"](mid=st[i - 1]["s3b2"])
                    st[i - 1]["s3b3"]()
                else:
                    st[i]["s2b"]()
                if i + 1 < nI:
                    st[i + 1]["s1"]()
                if i + 1 < nI:
                    st[i + 1]["s1b"]()
                if full:
                    st[i]["s3a"]()
                finish(i)
            if full:
                st[nI - 1]["s3b1"]()
                st[nI - 1]["s3b2"]()
                st[nI - 1]["s3b3"]()

        lb1, _ = ffn_loaders(w1g, w1u, w1d)
        if with_state_pass:
            load_x(xa)
            load_consts()
            norm_phase(0)
            ffn_phase(w1g, w1u, w1d, after_tile=lambda t: norm_tile(1, t), after_upgate=lambda: load_mixer_w(0, 0))
            load_x(xb)
            mixer_pass(True, csa_d, preloaded0=True, on_slot0_free=lambda: (lb1(0), lb1(1)))
        else:
            load_x(xb)
            load_consts()
        norm_phase(0)
        ffn_phase(w1g, w1u, w1d, after_tile=lambda t: norm_tile(1, t), after_upgate=lambda: load_mixer_w(0, 0),
                  preloaded=with_state_pass)
        wo_sb = mw[0]

        def load_wout():
            cdma(wo_sb, wout.rearrange("(k p) c -> p k c", p=128), [], [b_mw[0], b_wgu[0], b_wgu[1], b_ptb], chan=b_mw[0])

        mixer_pass(False, csb_d, preloaded0=True, on_slot0_free=load_wout)
        for t in range(NT):
            for dc in range(KC):
                bk = bank()
                for hc in range(KC):
                    mm(ps[bk][:], wo_sb[:, hc, dc * 128:(dc + 1) * 128], a[:, hc, tok(t)], hc == 0, hc == KC - 1,
                       [b_mw[0], b_a[hc][t]], [b_ps[bk]])
                tt("dve", h[:, dc, tok(t)], ps[bk][:], h[:, dc, tok(t)], ALU.add, [b_ps[bk], b_h[dc][t]], [b_h[dc][t]])
            norm_tile(2, t)
        wpg_sb = mw[0]
        ptb2 = carve(OFF_WD, 8192, BF16).rearrange("p (k t) -> p k t", k=2)
        wpp_sb = carve(OFF_WD + 8192, 4096, BF16).rearrange("p (k c) -> p k c", k=2)

        def prefetch_ple():
            cdma(wpg_sb, wpg.rearrange("(k p) c -> p k c", p=128), [], [b_mw[0], b_wgu[0], b_wgu[1], b_ptb],
                 chan=b_mw[0])

        ffn_phase(w2g, w2u, w2d, after_tile=lambda t: norm_tile(3, t), after_upgate=prefetch_ple)
        cdma(ptb2, pT.rearrange("(k p) t -> p k t", p=128), [], [b_wd, b_mw[1]], chan=b_wd)
        cdma(wpp_sb, wpp.rearrange("(k p) c -> p k c", p=128), [], [b_wd, b_mw[1]], chan=b_wd)
        for t in range(NT):
            for dc in range(KC):
                bg_ = bank()
                for k in range(KC):
                    mm(ps[bg_][:], wpg_sb[:, k, dc * 128:(dc + 1) * 128], n[:, k, tok(t)], k == 0, k == KC - 1,
                       [b_mw[0], b_n[k][t]], [b_ps[bg_]])
                be_ = bank()
                for k in range(2):
                    mm(ps[be_][:], wpp_sb[:, k, dc * 128:(dc + 1) * 128], ptb2[:, k, tok(t)], k == 0, k == 1,
                       [b_wd], [b_ps[be_]])
                fi = nxt("ple", 2)
                act(f32w[fi][:], ps[bg_][:], AF.Sigmoid, [b_ps[bg_]], [b_f32w[fi]])
                tt("dve", f32w[fi][:], f32w[fi][:], ps[be_][:], ALU.mult, [b_f32w[fi], b_ps[be_]], [b_f32w[fi]])
                tt("dve", h[:, dc, tok(t)], h[:, dc, tok(t)], f32w[fi][:], ALU.add, [b_f32w[fi], b_h[dc][t]],
                   [b_h[dc][t]])
            if t >= 1:
                norm_tile(4, t - 1, final=True)
        norm_tile(4, NT - 1, final=True)
        P.emit(final_wait_chans=b_stg)
    return nc, P


def _consts(pos0):
    half = 32
    inv = (10000.0 ** (-np.arange(half, dtype=np.float32) / half)).astype(np.float32)
    pos = (pos0 + np.arange(T)).astype(np.float32)
    ang = pos[:, None] * inv[None, :]
    cos = np.cos(ang).astype(np.float32)
    sin = np.sin(ang).astype(np.float32)
    cs = np.zeros((2, 128, T), np.float32)
    for hh in range(2):
        for d in range(64):
            cs[0, hh * 64 + d] = cos[:, d % 32]
            cs[1, hh * 64 + d] = (-sin[:, d % 32]) if d < 32 else sin[:, d % 32]
    return cs


def _static_tables():
    gam = (1.0 - 2.0 ** (-5.0 - np.arange(4, dtype=np.float64)))
    i = np.arange(64)
    m1 = (i[None, :] >= i[:, None]).astype(np.float32)
    mfb = np.zeros((128, 2, 64), np.float32)
    for pb in range(2):
        mfb[pb * 64:(pb + 1) * 64, 0, :] = m1
        mfb[pb * 64:(pb + 1) * 64, 1, :] = 1.0 - m1
    dm = np.zeros((128, 4, 64), np.float32)
    gq = np.zeros((128, 2, 64), np.float32)
    dectok = np.zeros((128, 2, 128), np.float32)
    dret = np.zeros((128, 2), np.float32)
    for hd in range(4):
        dist = np.abs(i[:, None] - i[None, :])
        msk = (0.125 * gam[hd] ** dist).astype(np.float32)
        for pb in range(2):
            dm[pb * 64:(pb + 1) * 64, hd, :] = msk
        r, hh = hd // 2, hd % 2
        qd = (0.125 * gam[hd] ** (i + 1.0)).astype(np.float32)
        gq[hh * 64:(hh + 1) * 64, r, :] = qd[None, :]
        kd = (gam[hd] ** (63.0 - i)).astype(np.float32)
        dectok[:, r, hh * 64:(hh + 1) * 64] = np.tile(kd, 2)[:, None]
        dret[hh * 64:(hh + 1) * 64, r] = np.float32(gam[hd] ** 64)
    mscan = np.ones((128, 512), np.float32)
    mscan[:, ::64] = 0.0
    ident = np.eye(128, dtype=np.float32).astype(ml_dtypes.bfloat16)
    return dict(mfb=mfb, dm=dm, gq=gq, dectok=dectok, dret=dret, mscan=mscan, ident=ident)


_CACHE = {}


def kernel(x, p, g_ffn1, w_ffn1_gate, w_ffn1_up, w_ffn1_down, g_mix, w_in, w_gla_gate_up, b_gla_gate, g_gla_out,
           g_ret_out, w_out, g_ffn2, w_ffn2_gate, w_ffn2_up, w_ffn2_down, g_ple, w_ple_gate, w_ple_proj, g_final):
    f = lambda v: np.ascontiguousarray(np.asarray(v, dtype=np.float32))
    x = f(x)
    p = f(p)
    if "nc" not in _CACHE:
        _CACHE["nc"] = build_program()[0]
    nc = _CACHE["nc"]
    gains = np.stack([f(g_ffn1)[0], f(g_mix)[0], f(g_ffn2)[0], f(g_ple)[0], f(g_final)], 0)
    gains = np.ascontiguousarray(gains.reshape(5, 8, 128).transpose(2, 0, 1).reshape(128, 40))
    gh = np.concatenate([f(g_gla_out)[0], f(g_ret_out)[0]], 0)
    ghead = np.ascontiguousarray(gh.reshape(8, 128).T)
    wupa = np.ascontiguousarray(np.concatenate([f(w_gla_gate_up)[0], f(b_gla_gate)[0][None, :]], 0))
    st = _static_tables()
    shared = dict(w1g=f(w_ffn1_gate)[0], w1u=f(w_ffn1_up)[0], w1d=f(w_ffn1_down)[0], w2g=f(w_ffn2_gate)[0],
                  w2u=f(w_ffn2_up)[0], w2d=f(w_ffn2_down)[0], win=f(w_in)[0], wupa=wupa, wout=f(w_out)[0],
                  wpg=f(w_ple_gate)[0], wpp=f(w_ple_proj)[0], gains=gains, ghead=ghead, **st)
    cs_tabs = [_consts(0), _consts(T)]
    in_maps = []
    for c in range(8):
        b, hf = c // 2, c % 2
        xb = np.ascontiguousarray(x[b, hf * T:(hf + 1) * T, :].T)
        xa = np.ascontiguousarray(x[b, 0:T, :].T)
        pT = np.ascontiguousarray(p[0, b, hf * T:(hf + 1) * T, :].T)
        m = dict(shared)
        m.update(xa=xa, xb=xb, pT=pT, flag=np.full((128, 1), float(hf), np.float32), csa=cs_tabs[0], csb=cs_tabs[hf])
        in_maps.append(m)
    res = run_bass_kernel_spmd(nc, in_maps, core_ids=list(range(8)))
    out = np.empty((4, SEQ, D), np.float32)
    for c in range(8):
        b, hf = c // 2, c % 2
        out[b, hf * T:(hf + 1) * T, :] = res.results[c]["outT"].T
    return out
```

```python
import math
from contextlib import ExitStack

import numpy as np
import ml_dtypes
import concourse.bass as bass
import concourse.mybir as mybir
from concourse.bass_utils import run_bass_kernel_spmd

F32 = mybir.dt.float32
BF16 = mybir.dt.bfloat16
U8 = mybir.dt.uint8
ALU = mybir.AluOpType
AF = mybir.ActivationFunctionType

ENGS = ["pe", "act", "dve", "pool", "sp"]


class Buf:
    __slots__ = ("name", "last_w", "readers", "sem", "cnt")

    def __init__(self, name):
        self.name = name
        self.last_w = None
        self.readers = []
        self.sem = None
        self.cnt = 0


class Op:
    __slots__ = ("eng", "fn", "deps", "is_dma", "chan", "sem", "val", "has_dep")

    def __init__(self, eng, fn, is_dma):
        self.eng = eng
        self.fn = fn
        self.deps = []
        self.is_dma = is_dma
        self.chan = None
        self.sem = None
        self.val = 0
        self.has_dep = False


class Prog:
    def __init__(self, nc, same_engine_sync=True):
        self.nc = nc
        self.streams = {e: [] for e in ENGS}
        self.same_engine_sync = same_engine_sync
        self.chans = []

    def _need_sync(self, a, b):
        if a.is_dma:
            if b.is_dma and a.chan is b.chan:
                return False
            return True
        if b.is_dma:
            return True
        if a.eng == b.eng:
            if a.eng == "pe":
                return False
            return self.same_engine_sync
        return True

    def _track(self, o, reads, writes):
        deps = []
        for r in reads:
            a = r.last_w
            if a is not None and a is not o and self._need_sync(a, o):
                deps.append(a)
        for w in writes:
            a = w.last_w
            if a is not None and a is not o and self._need_sync(a, o):
                deps.append(a)
            for a in w.readers:
                if a is not o and self._need_sync(a, o):
                    deps.append(a)
        for r in reads:
            r.readers.append(o)
        for w in writes:
            w.last_w = o
            w.readers = []
        seen = set()
        for a in deps:
            if id(a) in seen:
                continue
            seen.add(id(a))
            a.has_dep = True
            o.deps.append((a, a.chan.cnt if a.is_dma else None))

    def op(self, eng, fn, reads=(), writes=()):
        o = Op(eng, fn, False)
        self._track(o, reads, writes)
        self.streams[eng].append(o)
        return o

    def dma(self, eng, fn, reads=(), writes=(), chan=None):
        o = Op(eng, fn, True)
        if chan is None:
            chan = writes[0]
        o.chan = chan
        if chan.sem is None:
            chan.sem = "pending"
            self.chans.append(chan)
        chan.cnt += 16
        o.val = chan.cnt
        self._track(o, reads, writes)
        self.streams[eng].append(o)
        return o

    def emit(self, final_wait_chans=()):
        nc = self.nc
        with ExitStack() as es:
            esem = {e: es.enter_context(nc.semaphore("s_" + e)) for e in ENGS}
            for c in self.chans:
                c.sem = es.enter_context(nc.semaphore("d_" + c.name))
            for e in ENGS:
                t = 0
                for o in self.streams[e]:
                    if o.is_dma:
                        o.sem = o.chan.sem
                    elif o.has_dep:
                        t += 1
                        o.val = t
                        o.sem = esem[e]
            block = es.enter_context(nc.Block())

            def run(ename, eng):
                waited = {}
                for o in self.streams[ename]:
                    for d, dval in o.deps:
                        key = id(d.sem)
                        val = d.val if dval is None else dval
                        if waited.get(key, 0) >= val:
                            continue
                        eng.wait_ge(d.sem, val)
                        waited[key] = val
                    ins = o.fn(eng)
                    if o.is_dma:
                        ins.then_inc(o.sem, 16)
                    elif o.has_dep:
                        ins.then_inc(o.sem, 1)
                if ename == "sp":
                    for c in final_wait_chans:
                        eng.wait_ge(c.sem, c.cnt)

            @block.tensor
            def _(eng):
                run("pe", eng)

            @block.scalar
            def _(eng):
                run("act", eng)

            @block.vector
            def _(eng):
                run("dve", eng)

            @block.gpsimd
            def _(eng):
                run("pool", eng)

            @block.sync
            def _(eng):
                run("sp", eng)


D = 1024
SEQ = 4096
T = 2048
NT = 4
TW = 512
KC = 8
DFF = 2816
NBLK = 11
THIRDS = [(0, 4), (4, 8), (8, 11)]
PLE = 256
CH = 64
EPS = 1e-6
O_GQ, O_GK, O_GV, O_GR, O_GLR, O_RQ, O_RK, O_RV, O_RG = 0, 256, 512, 1024, 1536, 1552, 1808, 2064, 2576
IN_COLS = 3088

OFF_H = 0
OFF_N = 65536
OFF_A = 98304
OFF_WGU = 131072
OFF_WD = 147456
ARENA = 163840


def build_program(with_state_pass=True):
    nc = bass.Bass("TRN2", target_bir_lowering=False)
    P = Prog(nc, same_engine_sync=False)

    def din(name, shape, dt=F32):
        return nc.dram_tensor(name, list(shape), dt, kind="ExternalInput").ap()

    xa = din("xa", [D, T])
    xb = din("xb", [D, T])
    pT = din("pT", [PLE, T])
    w1g, w1u, w1d = din("w1g", [D, DFF]), din("w1u", [D, DFF]), din("w1d", [DFF, D])
    w2g, w2u, w2d = din("w2g", [D, DFF]), din("w2u", [D, DFF]), din("w2d", [DFF, D])
    win = din("win", [D, IN_COLS])
    wupa = din("wupa", [17, 256])
    wout = din("wout", [D, D])
    wpg = din("wpg", [D, D])
    wpp = din("wpp", [PLE, D])
    gains_d = din("gains", [128, 40])
    ghead_d = din("ghead", [128, 8])
    flag_d = din("flag", [128, 1])
    csa_d = din("csa", [2, 128, T])
    csb_d = din("csb", [2, 128, T])
    mfb_d = din("mfb", [128, 2, 64])
    dm_d = din("dm", [128, 4, 64])
    gq_d = din("gq", [128, 2, 64])
    dectok_d = din("dectok", [128, 2, 128])
    dret_d = din("dret", [128, 2])
    mscan_d = din("mscan", [128, 512])
    ident_d = din("ident", [128, 128], BF16)
    outT = nc.dram_tensor("outT", [D, T], F32, kind="ExternalOutput").ap()

    es = ExitStack()
    with es:
        def sb(name, shape, dt):
            return es.enter_context(nc.sbuf_tensor("sb_" + name, list(shape), dt))

        arena = sb("arena", [128, ARENA], U8)

        def carve(off, nbytes, dt):
            return arena[:, off:off + nbytes].bitcast(dt)

        h = carve(OFF_H, 65536, F32).rearrange("p (k t) -> p k t", k=KC)
        n = carve(OFF_N, 32768, BF16).rearrange("p (k t) -> p k t", k=KC)
        a = carve(OFF_A, 32768, BF16).rearrange("p (k t) -> p k t", k=KC)
        wgu = [carve(OFF_WGU + s * 8192, 8192, BF16).rearrange("p (g k c) -> p g k c", g=2, k=KC) for s in range(2)]
        wd = carve(OFF_WD, 16384, BF16).rearrange("p (k c) -> p k c", k=KC)
        stg = [carve(OFF_A + i * 2048, 2048, F32) for i in range(8)]
        mw = [carve(OFF_WGU, 16384, BF16).rearrange("p (k c) -> p k c", k=KC), wd]
        ptb = carve(OFF_WGU, 8192, BF16).rearrange("p (k t) -> p k t", k=2)

        gains = sb("gains", [128, 40], F32)
        ghead = sb("ghead", [128, 8], F32)
        flag = sb("flag", [128, 1], F32)
        mfb = sb("mfb", [128, 2, 64], F32)
        dm = sb("dm", [128, 4, 64], F32)
        gq = sb("gq", [128, 2, 64], F32)
        dectok = sb("dectok", [128, 2, 128], F32)
        dret = sb("dret", [128, 2], F32)
        mscan = sb("mscan", [128, 512], F32)
        ident = sb("ident", [128, 128], BF16)
        ones = sb("ones", [128, 128], BF16)
        wupa_sb = sb("wupa", [32, 256], F32)
        wglr = sb("wglr", [128, KC, 16], BF16)
        sqr = [sb(f"sq{i}", [128, TW], BF16) for i in range(2)]
        rstd = [sb(f"rstd{i}", [128, TW], F32) for i in range(1)]
        sil = [sb(f"sil{i}", [128, TW], BF16) for i in range(2)]
        f32all = sb("f32w", [128, 5, TW], F32)
        f32w = [f32all[:, i, :] for i in range(5)]
        b16w = [sb(f"b16w{i}", [128, TW], BF16) for i in range(7)]
        cs = f32all[:, 2:4, :]
        glra = sb("glra", [32, TW], F32)
        v_sbs = [sb(f"v_sb{i}", [128, 4, 256], BF16) for i in range(2)]
        qi_t = [sb(f"qi{i}", [128, TW], BF16) for i in range(2)]
        kst_sb = sb("kst_sb", [128, 4, 128], BF16)
        S_all1 = sb("S_all", [128, 9, 128], F32)
        S_all = [S_all1] * 4
        S16 = sb("S16", [128, 8, 128], BF16)
        S_in = sb("S_in", [128, 4, 128], F32)
        ps = [es.enter_context(nc.psum_tensor(f"ps{i}", [128, TW], F32)) for i in range(8)]

        b_h = [[Buf(f"h{k}_{t}") for t in range(NT)] for k in range(KC)]
        b_n = [[Buf(f"n{k}_{t}") for t in range(NT)] for k in range(KC)]
        b_a = [[Buf(f"a{k}_{t}") for t in range(NT)] for k in range(KC)]
        b_wgu = [Buf("wgu0"), Buf("wgu1")]
        b_wd = Buf("wd")
        b_mw = [Buf("mw0"), Buf("mw1")]
        b_ptb = Buf("ptb")
        b_ps = [Buf(f"ps{i}") for i in range(8)]
        b_sq = [Buf(f"sq{i}") for i in range(2)]
        b_rstd = [Buf(f"rstd{i}") for i in range(1)]
        b_sil = [Buf(f"sil{i}") for i in range(2)]
        b_f32w = [Buf(f"f32w{i}") for i in range(5)]
        b_b16w = [Buf(f"b16w{i}") for i in range(7)]
        b_glra = Buf("glra")
        b_vsb = [Buf("v_sb0"), Buf("v_sb1")]
        b_qi = [Buf("qi0"), Buf("qi1")]
        b_cs = Buf("cs")
        b_kst = Buf("kst_sb")
        b_S = [Buf("S")] * 4
        b_S16 = Buf("S16")
        b_Sin = Buf("S_in")
        b_const = Buf("const")
        b_out = Buf("out")
        b_stg = [Buf(f"stg{i}") for i in range(8)]
        cnt = {"bank": 0, "sq": 0, "rstd": 0, "sil": 0, "blk": 0, "ost": 0, "ple": 0}

        def nxt(key, mod):
            v = cnt[key] % mod
            cnt[key] += 1
            return v

        def bank():
            return nxt("bank", 8)

        def mm(out, lhsT, rhs, start, stop, reads, writes):
            P.op("pe", lambda e: e.matmul(out, lhsT=lhsT, rhs=rhs, start=start, stop=stop), reads, writes)

        def tr(out, in_, reads, writes):
            P.op("pe", lambda e: e.transpose(out, in_, ident[:]), list(reads) + [b_const], writes)

        def act(out, in_, func, reads, writes, scale=1.0, bias=0.0):
            P.op("act", lambda e: e.activation(out=out, in_=in_, func=func, bias=bias, scale=scale), reads, writes)

        def tt(eng, out, in0, in1, op, reads, writes):
            P.op(eng, lambda e: e.tensor_tensor(out=out, in0=in0, in1=in1, op=op), reads, writes)

        def stt(eng, out, in0, scalar, in1, op0, op1, reads, writes):
            P.op(eng, lambda e: e.scalar_tensor_tensor(out=out, in0=in0, scalar=scalar, in1=in1, op0=op0, op1=op1),
                 reads, writes)

        def cp(eng, out, in_, reads, writes):
            if eng == "act":
                P.op("act", lambda e: e.copy(out=out, in_=in_), reads, writes)
            else:
                P.op(eng, lambda e: e.tensor_copy(out=out, in_=in_), reads, writes)

        def tsm(eng, out, in0, scalar, reads, writes):
            P.op(eng, lambda e: e.tensor_scalar_mul(out=out, in0=in0, scalar1=scalar), reads, writes)

        def cdma(out, in_, reads, writes, chan=None):
            P.dma("pool", lambda e: e.dma_start(out=out, in_=in_), reads, writes, chan)

        def hdma(out, in_, reads, writes, chan=None):
            P.dma("sp", lambda e: e.dma_start(out=out, in_=in_), reads, writes, chan)

        def tok(t):
            return slice(t * TW, (t + 1) * TW)

        def load_consts():
            cbufs = [Buf(f"c{i}") for i in range(16)]
            consts = [(gains, gains_d), (ghead, ghead_d), (flag, flag_d), (mfb, mfb_d), (dm, dm_d), (gq, gq_d),
                      (dectok, dectok_d), (dret, dret_d), (mscan, mscan_d), (ident, ident_d)]
            for i, (s_, d_) in enumerate(consts):
                hdma(s_[:], d_, [], [cbufs[i]], chan=cbufs[i])
            P.op("pool", lambda e: e.memset(wupa_sb[:], 0.0), [], [cbufs[10]])
            hdma(wupa_sb[0:17, :], wupa, [], [cbufs[10]], chan=cbufs[10])
            P.op("pool", lambda e: e.memset(ones[:], 1.0), [], [cbufs[12]])
            cdma(wglr[:], win[:, O_GLR:O_GLR + 16].rearrange("(k p) c -> p k c", p=128), [], [cbufs[11]],
                 chan=cbufs[11])
            P.op("pool", lambda e: e.memset(glra[:], 1.0), cbufs[0:13], [b_glra, b_const])
            P.op("pool", lambda e: e.memset(S_in[:], 0.0), [], [b_Sin])
            P.op("pool", lambda e: e.memset(S_all1[:], 0.0), [], [b_S[0]])

        b_hl = [Buf(f"hl{t}") for t in range(NT)]

        def load_x(xsrc):
            xv = xsrc.rearrange("(k p) t -> p k t", p=128)
            for t in range(NT):
                hdma(h[:, :, tok(t)], xv[:, :, tok(t)], [], [b_hl[t]] + [b_h[k][t] for k in range(KC)], chan=b_hl[t])

        def norm_phase(gi, final=False):
            for t in range(NT):
                norm_tile(gi, t, final)

        def norm_tile(gi, t, final=False):
            if True:
                bi = bank()
                for k in range(KC):
                    si = nxt("sq", 2)
                    act(sqr[si][:], h[:, k, tok(t)], AF.Square, [b_h[k][t]], [b_sq[si]])
                    mm(ps[bi][:], ones[:], sqr[si][:], k == 0, k == KC - 1, [b_sq[si], b_const], [b_ps[bi]])
                ri = nxt("rstd", 1)
                act(rstd[ri][:], ps[bi][:], AF.Ln, [b_ps[bi]], [b_rstd[ri]], scale=1.0 / D, bias=EPS)
                act(rstd[ri][:], rstd[ri][:], AF.Exp, [b_rstd[ri]], [b_rstd[ri]], scale=-0.5)
                for k in range(KC):
                    g_ap = gains[:, gi * 8 + k:gi * 8 + k + 1]
                    if not final:
                        eng = "dve"
                        stt(eng, n[:, k, tok(t)], h[:, k, tok(t)], g_ap, rstd[ri][:], ALU.mult, ALU.mult,
                            [b_h[k][t], b_rstd[ri], b_const], [b_n[k][t]])
                    else:
                        stt("dve", stg[k], h[:, k, tok(t)], g_ap, rstd[ri][:], ALU.mult, ALU.mult,
                            [b_h[k][t], b_rstd[ri], b_const],
                            [b_stg[k], b_a[k // 2][2 * (k % 2)], b_a[k // 2][2 * (k % 2) + 1]])
                        hdma(outT[k * 128:(k + 1) * 128, tok(t)], stg[k], [b_stg[k]], [b_out], chan=b_stg[k])

        def ffn_loaders(wg, wu, wdn):
            def load_blk(b):
                s = b % 2
                cdma(wgu[s][:, 0], wg[:, b * 256:(b + 1) * 256].rearrange("(k p) c -> p k c", p=128), [],
                     [b_wgu[s], b_mw[0], b_ptb], chan=b_wgu[s])
                cdma(wgu[s][:, 1], wu[:, b * 256:(b + 1) * 256].rearrange("(k p) c -> p k c", p=128), [],
                     [b_wgu[s]], chan=b_wgu[s])

            def load_wd(k3):
                b0, b1 = THIRDS[k3]
                nch = (b1 - b0) * 2
                cdma(wd[:, 0:nch, :], wdn[b0 * 256:b1 * 256, :].rearrange("(c p) d -> p c d", p=128), [],
                     [b_wd, b_mw[1]], chan=b_wd)

            return load_blk, load_wd

        def ffn_phase(wg, wu, wdn, after_tile=None, after_upgate=None, preloaded=False):
            load_blk, load_wd = ffn_loaders(wg, wu, wdn)
            if not preloaded:
                load_blk(0)
                load_blk(1)
            load_wd(0)
            for k3, (b0, b1) in enumerate(THIRDS):
                nch = (b1 - b0) * 2
                for b in range(b0, b1):
                    s = b % 2
                    for j in range(2):
                        ci = (b - b0) * 2 + j
                        for t in range(NT):
                            bu = bank()
                            bv = bank()
                            for k in range(KC):
                                mm(ps[bu][:], wgu[s][:, 0, k, j * 128:(j + 1) * 128], n[:, k, tok(t)], k == 0,
                                   k == KC - 1, [b_wgu[s], b_n[k][t]], [b_ps[bu]])
                            for k in range(KC):
                                mm(ps[bv][:], wgu[s][:, 1, k, j * 128:(j + 1) * 128], n[:, k, tok(t)], k == 0,
                                   k == KC - 1, [b_wgu[s], b_n[k][t]], [b_ps[bv]])
                            si = nxt("sil", 2)
                            act(sil[si][:], ps[bu][:], AF.Silu, [b_ps[bu]], [b_sil[si]])
                            tt("dve", a[:, ci, tok(t)], sil[si][:], ps[bv][:], ALU.mult, [b_sil[si], b_ps[bv]],
                               [b_a[ci][t]])
                    if b + 2 < NBLK:
                        load_blk(b + 2)
                last = k3 + 1 == len(THIRDS)
                if last and after_upgate is not None:
                    after_upgate()
                order = [(dc, t) for t in range(NT) for dc in range(KC)] if last else \
                        [(dc, t) for dc in range(KC) for t in range(NT)]
                for dc, t in order:
                    bk = bank()
                    for ci in range(nch):
                        mm(ps[bk][:], wd[:, ci, dc * 128:(dc + 1) * 128], a[:, ci, tok(t)], ci == 0,
                           ci == nch - 1, [b_wd, b_a[ci][t]], [b_ps[bk]])
                    stt("dve", h[:, dc, tok(t)], ps[bk][:], 0.5, h[:, dc, tok(t)], ALU.mult, ALU.add,
                        [b_ps[bk], b_h[dc][t]], [b_h[dc][t]])
                    if last and dc == KC - 1 and after_tile is not None:
                        after_tile(t)
                if k3 + 1 < len(THIRDS):
                    load_wd(k3 + 1)

        def load_mixer_w(hp, slot):
            m = mw[slot]
            wr = [b_mw[slot]] + ([b_wgu[0], b_wgu[1], b_ptb] if slot == 0 else [b_wd])
            ch = b_mw[slot]

            def ld(c0, ncol, src0):
                cdma(m[:, :, c0:c0 + ncol], win[:, src0:src0 + ncol].rearrange("(k p) c -> p k c", p=128), [], wr,
                     chan=ch)

            if hp < 2:
                ld(0, 128, O_GQ + hp * 128)
                ld(128, 128, O_GK + hp * 128)
                ld(256, 256, O_GV + hp * 256)
                ld(512, 256, O_GR + hp * 256)
            else:
                r = hp - 2
                ld(0, 128, O_RQ + r * 128)
                ld(128, 128, O_RK + r * 128)
                ld(256, 256, O_RV + r * 256)
                ld(512, 256, O_RG + r * 256)
                for dst0, src0 in ((768, O_RQ + r * 128), (896, O_RK + r * 128)):
                    dview = m[:, :, dst0:dst0 + 128].rearrange("p k (hh two j) -> p k hh two j", hh=2, two=2)
                    sview = win[:, src0:src0 + 128].rearrange("(k p) (hh two j) -> p k hh two j", p=128, hh=2, two=2)
                    for two in range(2):
                        for hh in range(2):
                            P.dma("pool", lambda e, dv=dview[:, :, hh, two, :], sv=sview[:, :, hh, 1 - two, :]:
                                  e.dma_start(out=dv, in_=sv), [], wr, ch)

        def proj(m, slot, c0, t):
            bi = bank()
            for k in range(KC):
                mm(ps[bi][:], m[:, k, c0:c0 + 128], n[:, k, tok(t)], k == 0, k == KC - 1,
                   [b_mw[slot], b_n[k][t]], [b_ps[bi]])
            return bi

        def mixer_stages(it, hp, t, slot, state_only, cs_d):
            m = mw[slot]
            is_gla = hp < 2
            r = hp - 2
            W = b16w
            bW = b_b16w
            Fw = f32w
            bF = b_f32w
            par = it % 2
            vs = v_sbs[par]
            b_vs = b_vsb[par]
            QIt = qi_t[par]
            b_QI = b_qi[par]
            QA, KA, QB, KB, KST = 0, 1, 2, 3, 4
            SC0, SC1 = 5, 6
            Sa = S_all1
            bSa = b_S[0]
            def slot_of(c):
                return c if t % 2 == 0 else 8 - c

            if is_gla:
                def dec_ap(c, rows):
                    return Fw[3][rows, c * CH + CH - 1:c * CH + CH]
                b_dec = bF[3]
            else:
                def dec_ap(c, rows):
                    return dret[rows, r:r + 1]
                b_dec = b_const

            def s0():
                for half in range(2):
                    bi = bank()
                    for sub2 in range(2):
                        sub = half * 2 + sub2
                        for k in range(KC):
                            mm(ps[bi][:, sub2 * 256:(sub2 + 1) * 256],
                               n[:, k, t * TW + sub * 128:t * TW + (sub + 1) * 128],
                               m[:, k, 256:512], k == 0, k == KC - 1, [b_mw[slot], b_n[k][t]], [b_ps[bi]])
                    cp("act", vs[:, half * 2:half * 2 + 2, :], ps[bi][:].rearrange("p (s c) -> p s c", s=2),
                       [b_ps[bi]], [b_vs])
                if is_gla:
                    bg = bank()
                    for k in range(KC):
                        mm(ps[bg][0:16, :], wglr[:, k, :], n[:, k, tok(t)], k == 0, k == KC - 1, [b_const, b_n[k][t]],
                           [b_ps[bg]])
                    cp("act", glra[0:16, :], ps[bg][0:16, :], [b_ps[bg]], [b_glra])

            def s1():
                if is_gla:
                    bl = bank()
                    mm(ps[bl][:], wupa_sb[0:17, hp * 128:(hp + 1) * 128], glra[0:17, :], True, True,
                       [b_const, b_glra], [b_ps[bl]])
                    act(Fw[0][:], ps[bl][:], AF.Exp, [b_ps[bl]], [bF[0]], scale=-1.0)
                    act(Fw[0][:], Fw[0][:], AF.Ln, [bF[0]], [bF[0]], bias=1.0)
                    P.op("dve", lambda e: e.tensor_tensor_scan(out=Fw[1][:], data0=mscan[:], data1=Fw[0][:],
                                                               initial=0.0, op0=ALU.mult, op1=ALU.add),
                         [bF[0], b_const], [bF[1]])
                    cn3 = Fw[1][:].rearrange("p (c j) -> p c j", j=CH)
                    tt("dve", Fw[2][:].rearrange("p (c j) -> p c j", j=CH), cn3,
                       cn3[:, :, CH - 1:CH].to_broadcast([128, 8, CH]), ALU.subtract, [bF[1]], [bF[2]])
                    act(Fw[2][:], Fw[2][:], AF.Exp, [bF[2]], [bF[2]], scale=1.0 / 16)
                    act(Fw[3][:], Fw[1][:], AF.Exp, [bF[1]], [bF[3]], scale=-1.0 / 16)
                    if not state_only:
                        act(Fw[0][:], Fw[1][:], AF.Exp, [bF[1]], [bF[0]], scale=1.0 / 16)

            def s1b():
                if is_gla:
                    bk_ = proj(m, slot, 128, t)
                    tt("dve", W[KST][:], ps[bk_][:], Fw[2][:], ALU.mult, [b_ps[bk_], bF[2]], [bW[KST]])
                    if not state_only:
                        tt("dve", W[KA][:], ps[bk_][:], Fw[0][:], ALU.mult, [b_ps[bk_], bF[0]], [bW[KA]])
                        tt("dve", W[KB][:], ps[bk_][:], Fw[3][:], ALU.mult, [b_ps[bk_], bF[3]], [bW[KB]])
                        bq_ = proj(m, slot, 0, t)
                        stt("dve", QIt[:], ps[bq_][:], 0.125, Fw[3][:], ALU.mult, ALU.mult, [b_ps[bq_], bF[3]],
                            [b_QI])
                        stt("dve", W[QB][:], ps[bq_][:], 0.125, Fw[0][:], ALU.mult, ALU.mult, [b_ps[bq_], bF[0]],
                            [bW[QB]])
                else:
                    hdma(cs, cs_d[:, :, tok(t)].rearrange("a p t -> p a t"), [], [b_cs, bF[2], bF[3]], chan=b_cs)

                    def rope(c_main, c_perm, dst, bdst):
                        b1 = proj(m, slot, c_main, t)
                        tt("dve", Fw[0][:], ps[b1][:], cs[:, 0, :], ALU.mult, [b_ps[b1], b_cs], [bF[0]])
                        b2 = proj(m, slot, c_perm, t)
                        tt("dve", Fw[1][:], ps[b2][:], cs[:, 1, :], ALU.mult, [b_ps[b2], b_cs], [bF[1]])
                        tt("dve", dst[:], Fw[0][:], Fw[1][:], ALU.add, [bF[0], bF[1]], [bdst])

                    rope(128, 896, W[KA], bW[KA])
                    if not state_only:
                        rope(0, 768, W[QA], bW[QA])
                        tt("pool", QIt[:].rearrange("p (c j) -> p c j", j=CH),
                           W[QA][:].rearrange("p (c j) -> p c j", j=CH),
                           gq[:, r:r + 1, :].to_broadcast([128, 8, CH]), ALU.mult, [bW[QA], b_const], [b_QI])

            def s2a():
                kst_src = W[KST] if is_gla else W[KA]
                b_kst_src = bW[KST] if is_gla else bW[KA]
                if t == 0:
                    if state_only:
                        P.op("dve", lambda e: e.memset(Sa[:, 0, :], 0.0), [], [bSa])
                    else:
                        tsm("dve", Sa[:, 0, :], S_in[:, hp, :], flag[:, 0:1], [b_Sin, b_const], [bSa])
                bt = bank()
                psb = ps[bt][:].bitcast(BF16)
                for sub in range(4):
                    tr(psb[:, sub * 128:(sub + 1) * 128], kst_src[:, sub * 128:(sub + 1) * 128], [b_kst_src],
                       [b_ps[bt]])
                kview = psb[:, 0:512].rearrange("p (s c) -> p s c", s=4)
                if is_gla:
                    cp("dve", kst_sb[:], kview, [b_ps[bt]], [b_kst])
                else:
                    tt("dve", kst_sb[:], kview, dectok[:, r:r + 1, :].to_broadcast([128, 4, 128]), ALU.mult,
                       [b_ps[bt], b_const], [b_kst])
                if not state_only:
                    qsrc = QIt if is_gla else W[QA]
                    b_qsrc = b_QI if is_gla else bW[QA]
                    for hh in range(2):
                        bs = bank()
                        rows = slice(hh * 64, (hh + 1) * 64)
                        for c in range(8):
                            pb = (c % 2) * 64
                            col = (c // 2) * 64
                            cols = slice(c * CH, (c + 1) * CH)
                            mm(ps[bs][pb:pb + 64, col:col + 64], W[KA][rows, cols], qsrc[rows, cols], True, True,
                               [bW[KA], b_qsrc], [b_ps[bs]])
                            if is_gla:
                                mm(ps[bs][pb:pb + 64, 256 + col:256 + col + 64], W[KB][rows, cols], W[QB][rows, cols],
                                   True, True, [bW[KB], bW[QB]], [b_ps[bs]])
                        sct = W[SC0 + hh]
                        if is_gla:
                            tt("dve", sct[:].rearrange("p (d c i) -> p d c i", d=2, i=CH),
                               ps[bs][:].rearrange("p (d c i) -> p d c i", d=2, i=CH),
                               mfb[:].rearrange("p (d o) i -> p d o i", o=1).to_broadcast([128, 2, 4, CH]), ALU.mult,
                               [b_ps[bs], b_const], [bW[SC0 + hh]])
                        else:
                            hd = r * 2 + hh
                            tt("dve", sct[:, 0:256].rearrange("p (c i) -> p c i", i=CH),
                               ps[bs][:, 0:256].rearrange("p (c i) -> p c i", i=CH),
                               dm[:, hd:hd + 1, :].to_broadcast([128, 4, CH]), ALU.mult, [b_ps[bs], b_const],
                               [bW[SC0 + hh]])
            def s2b(mid=None):
                for rnd in range(2):
                    if rnd == 1 and mid is not None:
                        mid()
                    be, bo = bank(), bank()
                    for cc in range(4):
                        c = rnd * 4 + cc
                        pb = (c % 2) * 64
                        sub = c // 2
                        bU = be if c % 2 == 0 else bo
                        col = (cc // 2) * 256
                        mm(ps[bU][:, col:col + 256], kst_sb[pb:pb + 64, sub, :], vs[pb:pb + 64, sub, :], True, True,
                           [b_kst, b_vs], [b_ps[bU]])
                    for cc in range(4):
                        c = rnd * 4 + cc
                        bU = be if c % 2 == 0 else bo
                        col = (cc // 2) * 256
                        for hh in range(2):
                            rows = slice(hh * 64, (hh + 1) * 64)
                            stt("dve", Sa[rows, slot_of(c + 1), :], Sa[rows, slot_of(c), :], dec_ap(c, rows),
                                ps[bU][rows, col + hh * 128:col + (hh + 1) * 128], ALU.mult, ALU.add,
                                [bSa, b_ps[bU], b_dec], [bSa])
                if not state_only:
                    lo = 0 if t % 2 == 0 else 1
                    cp("act", S16[:], Sa[:, lo:lo + 8, :], [bSa], [b_S16])
                if state_only and t == NT - 1:
                    cp("pool", S_in[:, hp, :], Sa[:, 0, :], [bSa], [b_Sin])

            keep = {}

            def s3a():
                bie, bio = bank(), bank()
                for hh in range(2):
                    sct = W[SC0 + hh]
                    b_sct = bW[SC0 + hh]
                    for c in range(8):
                        pb = (c % 2) * 64
                        sub = c // 2
                        bI = bie if c % 2 == 0 else bio
                        ocol = hh * 256 + (c // 2) * 64
                        scol = (c // 2) * 64
                        lhs = vs[pb:pb + 64, sub, hh * 128:(hh + 1) * 128]
                        mm(ps[bI][:, ocol:ocol + 64], lhs, sct[pb:pb + 64, scol:scol + 64], True, not is_gla,
                           [b_vs, b_sct], [b_ps[bI]])
                        if is_gla:
                            mm(ps[bI][:, ocol:ocol + 64], lhs, sct[pb:pb + 64, 256 + scol:256 + scol + 64], False,
                               True, [b_vs, b_sct], [b_ps[bI]])
                bn = []
                for hh in range(2):
                    bn_ = bank()
                    bn.append(bn_)
                    rows = slice(hh * 64, (hh + 1) * 64)
                    for c in range(8):
                        mm(ps[bn_][:, c * CH:(c + 1) * CH], S16[rows, c if t % 2 == 0 else 7 - c, :],
                           QIt[rows, c * CH:(c + 1) * CH], True,
                           True, [b_S16, b_QI], [b_ps[bn_]])
                for hh in range(2):
                    ot, b_ot = (Fw[4], bF[4]) if hh == 0 else (rstd[0], b_rstd[0])
                    cp("act", ot[:], ps[bn[hh]][:], [b_ps[bn[hh]]], [b_ot])
                    o4 = ot[:].rearrange("p (cc two i) -> p cc two i", two=2, i=CH)
                    for pr, bI in ((0, bie), (1, bio)):
                        iv = ps[bI][:, hh * 256:(hh + 1) * 256].rearrange("p (cc i) -> p cc i", i=CH)
                        tt("dve", o4[:, :, pr, :], o4[:, :, pr, :], iv, ALU.add, [b_ot, b_ps[bI]], [b_ot])
                    si = nxt("sq", 2)
                    act(sqr[si][:], ot[:], AF.Square, [b_ot], [b_sq[si]])
                    keep["sq", hh] = si

            def sg():
                bgt = [proj(m, slot, 512 + hh * 128, t) for hh in range(2)]
                for hh in range(2):
                    head = hp * 2 + hh
                    act(a[:, head, tok(t)], ps[bgt[hh]][:], AF.Silu, [b_ps[bgt[hh]]], [b_a[head][t]])

            def s3b1():
                for hh in range(2):
                    si = keep["sq", hh]
                    bss = bank()
                    mm(ps[bss][:], ones[:], sqr[si][:], True, True, [b_sq[si], b_const], [b_ps[bss]])
                    act(ps[bss][:], ps[bss][:], AF.Ln, [b_ps[bss]], [b_ps[bss]], scale=1.0 / 128, bias=EPS)
                    act(ps[bss][:], ps[bss][:], AF.Exp, [b_ps[bss]], [b_ps[bss]], scale=-0.5)
                    keep["bss", hh] = bss

            def s3b2():
                for hh in range(2):
                    head = hp * 2 + hh
                    ot, b_ot = (Fw[4], bF[4]) if hh == 0 else (rstd[0], b_rstd[0])
                    bss = keep["bss", hh]
                    stt("dve", ot[:], ot[:], ghead[:, head:head + 1], ps[bss][:], ALU.mult, ALU.mult,
                        [b_ot, b_ps[bss], b_const], [b_ot])

            def s3b3():
                for hh in range(2):
                    head = hp * 2 + hh
                    ot, b_ot = (Fw[4], bF[4]) if hh == 0 else (rstd[0], b_rstd[0])
                    tt("pool", a[:, head, tok(t)], ot[:], a[:, head, tok(t)], ALU.mult, [b_ot, b_a[head][t]],
                       [b_a[head][t]])

            return dict(s0=s0, s1=s1, s1b=s1b, s2a=s2a, s2b=s2b, sg=sg, s3a=s3a, s3b1=s3b1, s3b2=s3b2, s3b3=s3b3)

        def mixer_pass(state_only, cs_d, preloaded0=False, on_slot0_free=None):
            if not preloaded0:
                load_mixer_w(0, 0)
            load_mixer_w(1, 1)
            its = [(hp, t) for hp in range(4) for t in range(NT)]
            st = [mixer_stages(i, hp, t, hp % 2, state_only, cs_d) for i, (hp, t) in enumerate(its)]
            nI = len(its)

            def finish(i):
                hp, t = its[i]
                if t == NT - 1 and hp + 2 < 4:
                    load_mixer_w(hp + 2, hp % 2)
                if t == NT - 1 and hp == 2 and on_slot0_free is not None:
                    on_slot0_free()

            st[0]["s0"]()
            st[0]["s1"]()
            st[0]["s1b"]()
            full = not state_only
            for i in range(nI):
                if i + 1 < nI:
                    st[i + 1]["s0"]()
                st[i]["s2a"]()
                if full:
                    st[i]["sg"]()
                if i >= 1 and full:
                    st[i - 1]["s3b1"]()
                if i >= 1 and full:
                    st[i]["s2b"](mid=st[i - 1]["s3b2"])
                    st[i - 1]["s3b3"]()
                else:
                    st[i]["s2b"]()
                if i + 1 < nI:
                    st[i + 1]["s1"]()
                if i + 1 < nI:
                    st[i + 1]["s1b"]()
                if full:
                    st[i]["s3a"]()
                finish(i)
            if full:
                st[nI - 1]["s3b1"]()
                st[nI - 1]["s3b2"]()
                st[nI - 1]["s3b3"]()

        lb1, _ = ffn_loaders(w1g, w1u, w1d)
        if with_state_pass:
            load_x(xa)
            load_consts()
            norm_phase(0)
            ffn_phase(w1g, w1u, w1d, after_tile=lambda t: norm_tile(1, t), after_upgate=lambda: load_mixer_w(0, 0))
            load_x(xb)
            mixer_pass(True, csa_d, preloaded0=True, on_slot0_free=lambda: (lb1(0), lb1(1)))
        else:
            load_x(xb)
            load_consts()
        norm_phase(0)
        ffn_phase(w1g, w1u, w1d, after_tile=lambda t: norm_tile(1, t), after_upgate=lambda: load_mixer_w(0, 0),
                  preloaded=with_state_pass)
        wo_sb = mw[0]

        def load_wout():
            cdma(wo_sb, wout.rearrange("(k p) c -> p k c", p=128), [], [b_mw[0], b_wgu[0], b_wgu[1], b_ptb], chan=b_mw[0])

        mixer_pass(False, csb_d, preloaded0=True, on_slot0_free=load_wout)
        for t in range(NT):
            for dc in range(KC):
                bk = bank()
                for hc in range(KC):
                    mm(ps[bk][:], wo_sb[:, hc, dc * 128:(dc + 1) * 128], a[:, hc, tok(t)], hc == 0, hc == KC - 1,
                       [b_mw[0], b_a[hc][t]], [b_ps[bk]])
                tt("dve", h[:, dc, tok(t)], ps[bk][:], h[:, dc, tok(t)], ALU.add, [b_ps[bk], b_h[dc][t]], [b_h[dc][t]])
            norm_tile(2, t)
        wpg_sb = mw[0]
        ptb2 = carve(OFF_WD, 8192, BF16).rearrange("p (k t) -> p k t", k=2)
        wpp_sb = carve(OFF_WD + 8192, 4096, BF16).rearrange("p (k c) -> p k c", k=2)

        def prefetch_ple():
            cdma(wpg_sb, wpg.rearrange("(k p) c -> p k c", p=128), [], [b_mw[0], b_wgu[0], b_wgu[1], b_ptb],
                 chan=b_mw[0])

        ffn_phase(w2g, w2u, w2d, after_tile=lambda t: norm_tile(3, t), after_upgate=prefetch_ple)
        cdma(ptb2, pT.rearrange("(k p) t -> p k t", p=128), [], [b_wd, b_mw[1]], chan=b_wd)
        cdma(wpp_sb, wpp.rearrange("(k p) c -> p k c", p=128), [], [b_wd, b_mw[1]], chan=b_wd)
        for t in range(NT):
            for dc in range(KC):
                bg_ = bank()
                for k in range(KC):
                    mm(ps[bg_][:], wpg_sb[:, k, dc * 128:(dc + 1) * 128], n[:, k, tok(t)], k == 0, k == KC - 1,
                       [b_mw[0], b_n[k][t]], [b_ps[bg_]])
                be_ = bank()
                for k in range(2):
                    mm(ps[be_][:], wpp_sb[:, k, dc * 128:(dc + 1) * 128], ptb2[:, k, tok(t)], k == 0, k == 1,
                       [b_wd], [b_ps[be_]])
                fi = nxt("ple", 2)
                act(f32w[fi][:], ps[bg_][:], AF.Sigmoid, [b_ps[bg_]], [b_f32w[fi]])
                tt("dve", f32w[fi][:], f32w[fi][:], ps[be_][:], ALU.mult, [b_f32w[fi], b_ps[be_]], [b_f32w[fi]])
                tt("dve", h[:, dc, tok(t)], h[:, dc, tok(t)], f32w[fi][:], ALU.add, [b_f32w[fi], b_h[dc][t]],
                   [b_h[dc][t]])
            if t >= 1:
                norm_tile(4, t - 1, final=True)
        norm_tile(4, NT - 1, final=True)
        P.emit(final_wait_chans=b_stg)
    return nc, P


def _consts(pos0):
    half = 32
    inv = (10000.0 ** (-np.arange(half, dtype=np.float32) / half)).astype(np.float32)
    pos = (pos0 + np.arange(T)).astype(np.float32)
    ang = pos[:, None] * inv[None, :]
    cos = np.cos(ang).astype(np.float32)
    sin = np.sin(ang).astype(np.float32)
    cs = np.zeros((2, 128, T), np.float32)
    for hh in range(2):
        for d in range(64):
            cs[0, hh * 64 + d] = cos[:, d % 32]
            cs[1, hh * 64 + d] = (-sin[:, d % 32]) if d < 32 else sin[:, d % 32]
    return cs


def _static_tables():
    gam = (1.0 - 2.0 ** (-5.0 - np.arange(4, dtype=np.float64)))
    i = np.arange(64)
    m1 = (i[None, :] >= i[:, None]).astype(np.float32)
    mfb = np.zeros((128, 2, 64), np.float32)
    for pb in range(2):
        mfb[pb * 64:(pb + 1) * 64, 0, :] = m1
        mfb[pb * 64:(pb + 1) * 64, 1, :] = 1.0 - m1
    dm = np.zeros((128, 4, 64), np.float32)
    gq = np.zeros((128, 2, 64), np.float32)
    dectok = np.zeros((128, 2, 128), np.float32)
    dret = np.zeros((128, 2), np.float32)
    for hd in range(4):
        dist = np.abs(i[:, None] - i[None, :])
        msk = (0.125 * gam[hd] ** dist).astype(np.float32)
        for pb in range(2):
            dm[pb * 64:(pb + 1) * 64, hd, :] = msk
        r, hh = hd // 2, hd % 2
        qd = (0.125 * gam[hd] ** (i + 1.0)).astype(np.float32)
        gq[hh * 64:(hh + 1) * 64, r, :] = qd[None, :]
        kd = (gam[hd] ** (63.0 - i)).astype(np.float32)
        dectok[:, r, hh * 64:(hh + 1) * 64] = np.tile(kd, 2)[:, None]
        dret[hh * 64:(hh + 1) * 64, r] = np.float32(gam[hd] ** 64)
    mscan = np.ones((128, 512), np.float32)
    mscan[:, ::64] = 0.0
    ident = np.eye(128, dtype=np.float32).astype(ml_dtypes.bfloat16)
    return dict(mfb=mfb, dm=dm, gq=gq, dectok=dectok, dret=dret, mscan=mscan, ident=ident)


_CACHE = {}


def kernel(x, p, g_ffn1, w_ffn1_gate, w_ffn1_up, w_ffn1_down, g_mix, w_in, w_gla_gate_up, b_gla_gate, g_gla_out,
           g_ret_out, w_out, g_ffn2, w_ffn2_gate, w_ffn2_up, w_ffn2_down, g_ple, w_ple_gate, w_ple_proj, g_final):
    f = lambda v: np.ascontiguousarray(np.asarray(v, dtype=np.float32))
    x = f(x)
    p = f(p)
    if "nc" not in _CACHE:
        _CACHE["nc"] = build_program()[0]
    nc = _CACHE["nc"]
    gains = np.stack([f(g_ffn1)[0], f(g_mix)[0], f(g_ffn2)[0], f(g_ple)[0], f(g_final)], 0)
    gains = np.ascontiguousarray(gains.reshape(5, 8, 128).transpose(2, 0, 1).reshape(128, 40))
    gh = np.concatenate([f(g_gla_out)[0], f(g_ret_out)[0]], 0)
    ghead = np.ascontiguousarray(gh.reshape(8, 128).T)
    wupa = np.ascontiguousarray(np.concatenate([f(w_gla_gate_up)[0], f(b_gla_gate)[0][None, :]], 0))
    st = _static_tables()
    shared = dict(w1g=f(w_ffn1_gate)[0], w1u=f(w_ffn1_up)[0], w1d=f(w_ffn1_down)[0], w2g=f(w_ffn2_gate)[0],
                  w2u=f(w_ffn2_up)[0], w2d=f(w_ffn2_down)[0], win=f(w_in)[0], wupa=wupa, wout=f(w_out)[0],
                  wpg=f(w_ple_gate)[0], wpp=f(w_ple_proj)[0], gains=gains, ghead=ghead, **st)
    cs_tabs = [_consts(0), _consts(T)]
    in_maps = []
    for c in range(8):
        b, hf = c // 2, c % 2
        xb = np.ascontiguousarray(x[b, hf * T:(hf + 1) * T, :].T)
        xa = np.ascontiguousarray(x[b, 0:T, :].T)
        pT = np.ascontiguousarray(p[0, b, hf * T:(hf + 1) * T, :].T)
        m = dict(shared)
        m.update(xa=xa, xb=xb, pT=pT, flag=np.full((128, 1), float(hf), np.float32), csa=cs_tabs[0], csb=cs_tabs[hf])
        in_maps.append(m)
    res = run_bass_kernel_spmd(nc, in_maps, core_ids=list(range(8)))
    out = np.empty((4, SEQ, D), np.float32)
    for c in range(8):
        b, hf = c // 2, c % 2
        out[b, hf * T:(hf + 1) * T, :] = res.results[c]["outT"].T
    return out
```
